# Optimizing a Trainium2 kernel written in Bass

```python
import jax, jax.numpy as jnp
from jax import lax
import numpy as np

D_MODEL = 1024
BATCH = 8
SEQ = 8192
DEPTH = 1
DEC_BATCH = 4
DEC_SEQ = 8192
PAST_LEN = 128

D_CONV = D_MODEL
CONV_WIDTH = 3
GLA_HEADS = 4
GLA_DK = D_MODEL // 2
GLA_DV = D_MODEL
HEAD_K = GLA_DK // GLA_HEADS
HEAD_V = GLA_DV // GLA_HEADS
GATE_RANK = 16
GATE_TEMP = 16.0
CHUNK = 64
EPS = 1e-6

SPLIT_SIZES = (D_CONV, D_CONV, D_CONV, D_CONV,
               GLA_DK, GLA_DK, GLA_DV, GLA_DV,
               GATE_RANK, GATE_RANK,
               D_MODEL, D_MODEL)
N_IN = sum(SPLIT_SIZES)

kernel_name = 'hybrid_conv_gla_bidir_encoder'


def _split_points():
    return [int(v) for v in np.cumsum(SPLIT_SIZES)[:-1]]


def rmsnorm(x, g):
    xf = x.astype(jnp.float32)
    y = xf * lax.rsqrt(jnp.mean(xf * xf, axis=-1, keepdims=True) + EPS)
    return (y * g.astype(jnp.float32)).astype(x.dtype)


def depthwise_conv(u, w, b):
    c = u.shape[-1]
    out = lax.conv_general_dilated(
        u, w.astype(u.dtype)[:, None, :], window_strides=(1,),
        padding=[((CONV_WIDTH - 1) // 2, CONV_WIDTH // 2)],
        dimension_numbers=('NWC', 'WIO', 'NWC'), feature_group_count=c)
    return out + b.astype(u.dtype)


def gla_direction(q, k, v, g, strict):
    b, h, l, dk = q.shape
    dv = v.shape[-1]
    n = l // CHUNK

    def split(t):
        return t.reshape(b, h, n, CHUNK, t.shape[-1]).transpose(2, 0, 1, 3, 4)

    qc, kc, vc, gc = split(q), split(k), split(v), split(g)
    Gc = jnp.cumsum(gc, axis=-2)
    idx = jnp.arange(CHUNK)
    mask = (idx[:, None] > idx[None, :]) if strict else (idx[:, None] >= idx[None, :])

    def step(S, inp):
        qi, ki, vi, Gi = inp
        diff = Gi[..., :, None, :] - Gi[..., None, :, :]
        decay = jnp.exp(jnp.where(mask[:, :, None], diff, -jnp.inf))
        A = jnp.einsum('bhtsd,bhsd->bhts', qi[..., :, None, :] * decay, ki)
        o = (jnp.einsum('bhts,bhsv->bhtv', A, vi)
             + jnp.einsum('bhtd,bhdv->bhtv', qi * jnp.exp(Gi), S))
        G_last = Gi[..., -1:, :]
        S_new = (jnp.exp(G_last[..., 0, :])[..., None] * S
                 + jnp.einsum('bhsd,bhsv->bhdv', ki * jnp.exp(G_last - Gi), vi))
        return S_new, o

    S0 = jnp.zeros((b, h, dk, dv), jnp.float32)
    _, oc = lax.scan(step, S0, (qc, kc, vc, Gc))
    return oc.transpose(1, 2, 0, 3, 4).reshape(b, h, l, dv)


def encoder_layer(x, c, w_ada, b_ada, norm_g, w_in, conv_w, conv_b,
                  w_gate_f, b_gate_f, w_gate_b, b_gate_b, gla_norm_g,
                  w_a_out, w_b_out, w_out):
    bsz, l, _ = x.shape
    mod = jax.nn.silu(c) @ w_ada + b_ada
    shift, scale, gate = jnp.split(mod, 3, axis=-1)
    h = rmsnorm(x, norm_g) * (1.0 + scale[:, None, :]) + shift[:, None, :]

    proj = h @ w_in
    (a_b, a_c, a_x, a_z, q, k, v, b_z, lr_f, lr_b, m_a, m_b) = jnp.split(proj, _split_points(), axis=-1)

    u = depthwise_conv(a_c * a_x, conv_w, conv_b)
    y_a = (a_b * u * jax.nn.silu(a_z)) @ w_a_out

    def heads(t, d):
        return t.reshape(bsz, l, GLA_HEADS, d).transpose(0, 2, 1, 3).astype(jnp.float32)

    qh = heads(q, HEAD_K) * (HEAD_K ** -0.5)
    kh = heads(k, HEAD_K)
    vh = heads(v, HEAD_V)
    g_f = heads(jax.nn.log_sigmoid((lr_f @ w_gate_f + b_gate_f).astype(jnp.float32)) / GATE_TEMP, HEAD_K)
    g_b = heads(jax.nn.log_sigmoid((lr_b @ w_gate_b + b_gate_b).astype(jnp.float32)) / GATE_TEMP, HEAD_K)
    flip = lambda t: jnp.flip(t, axis=2)
    o = (gla_direction(qh, kh, vh, g_f, False)
         + flip(gla_direction(flip(qh), flip(kh), flip(vh), flip(g_b), True)))
    o = o * lax.rsqrt(jnp.mean(o * o, axis=-1, keepdims=True) + EPS)
    o = o.transpose(0, 2, 1, 3).reshape(bsz, l, GLA_DV) * gla_norm_g.astype(jnp.float32)
    y_b = (o.astype(x.dtype) * jax.nn.silu(b_z)) @ w_b_out

    merged = jax.nn.sigmoid(m_a) * y_a + jax.nn.sigmoid(m_b) * y_b
    return x + gate[:, None, :] * (merged @ w_out)


def encoder(x, c, w_ada, b_ada, norm_g, w_in, conv_w, conv_b, w_gate_f, b_gate_f,
            w_gate_b, b_gate_b, gla_norm_g, w_a_out, w_b_out, w_out, final_norm_g):
    for i in range(DEPTH):
        x = encoder_layer(x, c, w_ada[i], b_ada[i], norm_g[i], w_in[i], conv_w[i], conv_b[i],
                          w_gate_f[i], b_gate_f[i], w_gate_b[i], b_gate_b[i], gla_norm_g[i],
                          w_a_out[i], w_b_out[i], w_out[i])
    return rmsnorm(x, final_norm_g)


def setup_inputs(seed: int = 0) -> dict:
    key = jax.random.key(seed)
    ks = jax.random.split(key, 24)
    f32 = jnp.float32
    nrm = lambda k, shape, s: jax.random.normal(k, shape, f32) * s
    return {
        'x_prompt': nrm(ks[0], (BATCH, SEQ, D_MODEL), 1.0),
        'x_sample': nrm(ks[1], (DEC_BATCH, DEC_SEQ, D_MODEL), 1.0),
        'c_prompt': nrm(ks[2], (BATCH, D_MODEL), 1.0),
        'c_sample': nrm(ks[3], (DEC_BATCH, D_MODEL), 1.0),
        'w_ada': nrm(ks[4], (DEPTH, D_MODEL, 3 * D_MODEL), 0.5 * D_MODEL ** -0.5),
        'b_ada': nrm(ks[5], (DEPTH, 3 * D_MODEL), 0.02),
        'norm_g': 1.0 + nrm(ks[6], (DEPTH, D_MODEL), 0.02),
        'w_in': nrm(ks[7], (DEPTH, D_MODEL, N_IN), D_MODEL ** -0.5),
        'conv_w': nrm(ks[8], (DEPTH, CONV_WIDTH, D_CONV), CONV_WIDTH ** -0.5),
        'conv_b': nrm(ks[9], (DEPTH, D_CONV), 0.02),
        'w_gate_f': nrm(ks[10], (DEPTH, GATE_RANK, GLA_DK), GATE_RANK ** -0.5),
        'b_gate_f': 1.0 + nrm(ks[11], (DEPTH, GLA_DK), 0.5),
        'w_gate_b': nrm(ks[12], (DEPTH, GATE_RANK, GLA_DK), GATE_RANK ** -0.5),
        'b_gate_b': 1.0 + nrm(ks[13], (DEPTH, GLA_DK), 0.5),
        'gla_norm_g': 1.0 + nrm(ks[14], (DEPTH, GLA_DV), 0.02),
        'w_a_out': nrm(ks[15], (DEPTH, D_CONV, D_MODEL), D_CONV ** -0.5),
        'w_b_out': nrm(ks[16], (DEPTH, GLA_DV, D_MODEL), GLA_DV ** -0.5),
        'w_out': nrm(ks[17], (DEPTH, D_MODEL, D_MODEL), D_MODEL ** -0.5),
        'final_norm_g': 1.0 + nrm(ks[18], (D_MODEL,), 0.02),
    }


def reference(x_prompt, x_sample, c_prompt, c_sample, w_ada, b_ada, norm_g, w_in, conv_w, conv_b,
              w_gate_f, b_gate_f, w_gate_b, b_gate_b, gla_norm_g, w_a_out, w_b_out, w_out,
              final_norm_g):
    y_prompt = encoder(x_prompt, c_prompt, w_ada, b_ada, norm_g, w_in, conv_w, conv_b,
                       w_gate_f, b_gate_f, w_gate_b, b_gate_b, gla_norm_g, w_a_out, w_b_out,
                       w_out, final_norm_g)
    y_sample = encoder(x_sample, c_sample, w_ada, b_ada, norm_g, w_in, conv_w, conv_b,
                       w_gate_f, b_gate_f, w_gate_b, b_gate_b, gla_norm_g, w_a_out, w_b_out,
                       w_out, final_norm_g)
    return (y_prompt, y_sample)
```

```python
import numpy as np
from contextlib import ExitStack
import concourse.bass as bass
import concourse.mybir as mybir
from concourse.bass_utils import run_bass_kernel_spmd

F32 = mybir.dt.float32
BF16 = mybir.dt.bfloat16
AF = mybir.ActivationFunctionType
ALU = mybir.AluOpType

import os
CONV_INLINE = 0
D = 1024
NIN = 9248
EPS = 1e-6
ENGS = ['pe', 'act', 'dve', 'pool', 'sp']
C_AB, C_AC, C_AX, C_AZ, C_Q, C_K, C_V, C_BZ, C_LR, C_MA, C_MB = (
    0, 1024, 2048, 3072, 4096, 4608, 5120, 6144, 7168, 7200, 8224)


class Prog:
    def __init__(self, nc, stack):
        self.nc = nc
        self.stack = stack
        self.ops = {e: [] for e in ENGS}
        self.last_w = {}
        self.readers = {}
        self.dsem_total = {}
        self.dsems = {}
        self.waited_c = {e: {p: -1 for p in ENGS} for e in ENGS}
        self.waited_d = {e: {} for e in ENGS}
        self.csem = {e: stack.enter_context(nc.semaphore('cs_' + e)) for e in ENGS}

    def dsem(self, name):
        if name not in self.dsems:
            self.dsems[name] = self.stack.enter_context(self.nc.semaphore('ds_' + name))
            self.dsem_total[name] = 0
        return name

    def _collect(self, eng, tok, cand):
        if tok is None:
            return
        if tok[0] == 'c':
            _, peng, idx = tok
            if peng == eng and eng == 'pe':
                return
            key = ('c', peng)
            if cand.get(key, -1) < idx:
                cand[key] = idx
        else:
            _, sname, val = tok
            key = ('d', sname)
            if cand.get(key, 0) < val:
                cand[key] = val

    def _flush(self, eng, cand, waits):
        for key, v in cand.items():
            if key[0] == 'c':
                peng = key[1]
                if self.waited_c[eng][peng] >= v:
                    continue
                self.waited_c[eng][peng] = v
                self.ops[peng][v]['flag'] = True
                waits.append(('c', peng, v))
            else:
                sname = key[1]
                if self.waited_d[eng].get(sname, 0) >= v:
                    continue
                self.waited_d[eng][sname] = v
                waits.append(('d', sname, v))

    def op(self, eng, fn, reads=(), writes=(), dma=None):
        cand = {}
        for r in reads:
            self._collect(eng, self.last_w.get(r), cand)
        for w in writes:
            self._collect(eng, self.last_w.get(w), cand)
            for t in self.readers.get(w, []):
                self._collect(eng, t, cand)
        idx = len(self.ops[eng])
        rec = dict(fn=fn, waits=None, flag=False, dma=None)
        if dma is not None:
            self.dsem(dma)
            prev = self.dsem_total[dma]
            if prev > 0:
                self._collect(eng, ('d', dma, prev), cand)
            self.dsem_total[dma] = prev + 16
            tok = ('d', dma, prev + 16)
            rec['dma'] = dma
        else:
            tok = ('c', eng, idx)
        waits = []
        self._flush(eng, cand, waits)
        rec['waits'] = waits
        self.ops[eng].append(rec)
        for r in reads:
            self.readers.setdefault(r, []).append(tok)
        for w in writes:
            self.last_w[w] = tok
            self.readers[w] = []
        return tok

    def wait_all_dma(self, eng, names):
        waits = []
        cand = {}
        for n in names:
            if n in self.dsem_total and self.dsem_total[n] > 0:
                self._collect(eng, ('d', n, self.dsem_total[n]), cand)
        self._flush(eng, cand, waits)
        self.ops[eng].append(dict(fn=None, waits=waits, flag=False, dma=None))

    def emit(self):
        nc = self.nc
        cum = {}
        for e in ENGS:
            c = 0
            arr = []
            for o in self.ops[e]:
                if o['flag']:
                    c += 1
                arr.append(c)
            cum[e] = arr

        def run(eng_name, eng):
            for o in self.ops[eng_name]:
                for w in o['waits']:
                    if w[0] == 'c':
                        eng.wait_ge(self.csem[w[1]], cum[w[1]][w[2]])
                    else:
                        eng.wait_ge(self.dsems[w[1]], w[2])
                if o['fn'] is None:
                    continue
                ins = o['fn'](eng)
                if o['dma'] is not None:
                    ins.then_inc(self.dsems[o['dma']], 16)
                elif o['flag']:
                    ins.then_inc(self.csem[eng_name], 1)

        with nc.Block() as block:
            @block.tensor
            def _(e):
                run('pe', e)

            @block.scalar
            def _(e):
                run('act', e)

            @block.vector
            def _(e):
                run('dve', e)

            @block.gpsimd
            def _(e):
                run('pool', e)

            @block.sync
            def _(e):
                run('sp', e)


def build(L):
    HL = L // 2
    NTW = L // 512
    NTH = HL // 512
    nc = bass.Bass("TRN2", target_bir_lowering=False)

    def dram(name, shape, dtype, kind):
        return nc.dram_tensor(name, shape, dtype, kind=kind).ap()

    xw = dram("xw", [L + 256, D], F32, "ExternalInput")
    xh = dram("xh", [HL + 256, D], F32, "ExternalInput")
    xo = dram("xo", [HL, D], F32, "ExternalInput")
    cT_d = dram("cT", [128, 16], F32, "ExternalInput")
    flags_d = dram("flags", [128, 8], F32, "ExternalInput")
    pvec_d = dram("pvec", [128, 48], F32, "ExternalInput")
    bada_d = dram("bada", [128, 3 * D], F32, "ExternalInput")
    wada_d = dram("w_ada", [D, 3 * D], F32, "ExternalInput")
    win_d = dram("w_in", [D, NIN], F32, "ExternalInput")
    wa_d = dram("w_a_out", [D, D], F32, "ExternalInput")
    wb_d = dram("w_b_out", [D, D], F32, "ExternalInput")
    wo_d = dram("w_out", [D, D], F32, "ExternalInput")
    wg_d = dram("wg", [128, 3 * 512], F32, "ExternalInput")
    wlr_d = dram("wlr", [D, 128], F32, "ExternalInput")
    fg_d = dram("fg", [128, D], F32, "ExternalInput")
    yw = dram("yw", [L, D], F32, "ExternalOutput")
    yh = dram("yh", [HL, D], F32, "ExternalOutput")
    wbf_in = dram("wbf_in", [D, NIN], BF16, "Internal")
    wbf_a = dram("wbf_a", [D, D], BF16, "Internal")
    wbf_b = dram("wbf_b", [D, D], BF16, "Internal")
    wbf_o = dram("wbf_o", [D, D], BF16, "Internal")
    wbf_ada = dram("wbf_ada", [D, 3 * D], BF16, "Internal")
    sbw_d = dram("sbw", [L // 128, 128, D], BF16, "Internal")
    sbh_d = dram("sbh", [HL // 128, 128, D], BF16, "Internal")
    ksw_d = dram("ksw", [NTW, 128, 4 * 512], BF16, "Internal")
    vsw_d = dram("vsw", [NTW, 128, 4 * D], BF16, "Internal")
    ksh_d = dram("ksh", [NTH, 128, 4 * 512], BF16, "Internal")
    vsh_d = dram("vsh", [NTH, 128, 4 * D], BF16, "Internal")

    with ExitStack() as st:
        P = Prog(nc, st)

        def sb(name, shape, dt):
            return st.enter_context(nc.sbuf_tensor(name, shape, dt))

        def ps(name, shape, dt):
            return st.enter_context(nc.psum_tensor(name, shape, dt))

        NWB = 4
        wbuf = [sb("wbuf%d" % i, [128, 8, 512], BF16) for i in range(NWB)]
        xp = [sb("xp%d" % i, [128, D], F32) for i in range(2)]
        xr = [sb("xr%d" % i, [128, D], F32) for i in range(2)]
        hT = [sb("hT%d" % i, [128, 8, 513], BF16) for i in range(2)]
        xn = [sb("xn%d" % i, [128, D], BF16) for i in range(2)]
        junk = sb("junk", [128, D], BF16)
        stat = sb("stat", [128, 64], F32)
        pbuf = sb("pbuf", [128, 8, 514], BF16)
        B = [sb("B%d" % i, [128, 8, 512], BF16) for i in range(4)]
        ct = [sb("ct%d" % i, [128, 512], F32) for i in range(3)]
        cu = [sb("cu%d" % i, [128, 512], F32) for i in range(2)]
        qT = sb("qT", [128, 4, 512], BF16)
        kT = sb("kT", [128, 4, 512], BF16)
        vsb = sb("vsb", [128, 4, D], BF16)
        lrT = sb("lrT", [128, 512], BF16)
        res = [sb("res%d" % i, [128, D], F32) for i in range(2)]
        E4 = [sb("E%d" % i, [128, 4, 128], F32) for i in range(4)]
        qk4 = [sb("qk%d" % i, [128, 4, 128], BF16) for i in range(4)]
        tmpA = sb("tmpA", [128, 4, 128], F32)
        Abf = sb("Abf", [128, 4, 128], BF16)
        kTT = sb("kTT", [128, 4, 128], BF16)
        onb = sb("onb", [128, D], BF16)
        tmpS = sb("tmpS", [128, D], F32)
        Sf = sb("Sf", [128, D], F32)
        Sb = sb("Sb", [128, D], F32)
        Sfbf = sb("Sfbf", [128, D], BF16)
        Sbbf = [sb("Sbbf%d" % i, [128, D], BF16) for i in range(2)]
        ident = sb("ident", [128, 128], BF16)
        identf = sb("identf", [128, 128], F32)
        triF = sb("triF", [128, 128], F32)
        triB = sb("triB", [128, 128], F32)
        maskF = sb("maskF", [128, 4, 128], BF16)
        maskB = sb("maskB", [128, 4, 128], BF16)
        wg = sb("wgs", [128, 3, 512], BF16)
        wlr = sb("wlrs", [128, 8, 128], BF16)
        gate_bc = sb("gate_bc", [128, D], F32)
        fg_bc = sb("fg_bc", [128, D], F32)
        pvec = sb("pvec_s", [128, 48], F32)
        flags = sb("flags_s", [128, 8], F32)
        cTs = sb("cTs", [128, 16], F32)
        e0row = sb("e0row", [128, 128], BF16)
        modv = sb("modv", [128, 2, 16], F32)
        ones = sb("ones", [128, 128], F32)

        tmpb = cu[0]
        modbc = cu[1]
        csb = onb[:].rearrange("p (a b) -> p a b", a=8)
        badas = Abf[:].rearrange("p a b -> p (a b)")
        pj = [ps("pj%d" % i, [128, 512], F32) for i in range(3)]
        pT = ps("pT", [128, 8, 128], BF16)
        pa = [ps("pa%d" % i, [128, 4, 128], F32) for i in range(2)]
        po = ps("po", [128, D], F32)

        op = P.op
        rot = {}

        def nxt(key, n):
            v = rot.get(key, 0)
            rot[key] = v + 1
            return v % n

        def mm(out, lhsT, rhs, start, stop, reads, writes):
            op('pe', lambda e: e.matmul(out, lhsT=lhsT, rhs=rhs, start=start, stop=stop), reads=reads, writes=writes)

        def tr(out, in_, reads, writes, idn=None):
            idn = ident if idn is None else idn
            op('pe', lambda e: e.transpose(out=out, in_=in_, identity=idn[:]), reads=list(reads) + ['const'],
               writes=writes)

        def act(out, in_, func, reads, writes, scale=1.0, bias=0.0, accum=None):
            if accum is None:
                op('act', lambda e: e.activation(out=out, in_=in_, func=func, scale=scale, bias=bias),
                   reads=reads, writes=writes)
            else:
                op('act', lambda e: e.activation(out=out, in_=in_, func=func, scale=scale, bias=bias,
                                                 accum_out=accum), reads=reads, writes=writes)

        def tt(eng, out, in0, in1, alu, reads, writes):
            op(eng, lambda e: e.tensor_tensor(out=out, in0=in0, in1=in1, op=alu), reads=reads, writes=writes)

        def ts(eng, out, in0, s1, s2, op0, op1, reads, writes):
            if s2 is None:
                op(eng, lambda e: e.tensor_scalar(out=out, in0=in0, scalar1=s1, scalar2=None, op0=op0),
                   reads=reads, writes=writes)
            else:
                op(eng, lambda e: e.tensor_scalar(out=out, in0=in0, scalar1=s1, scalar2=s2, op0=op0, op1=op1),
                   reads=reads, writes=writes)

        def stt(eng, out, in0, scalar, in1, op0, op1, reads, writes):
            op(eng, lambda e: e.scalar_tensor_tensor(out=out, in0=in0, scalar=scalar, in1=in1, op0=op0, op1=op1),
               reads=reads, writes=writes)

        def cp(eng, out, in_, reads, writes):
            if eng == 'act':
                act(out, in_, AF.Copy, reads, writes)
            else:
                op(eng, lambda e: e.tensor_copy(out=out, in_=in_), reads=reads, writes=writes)

        def dma(eng, out, in_, sem, reads, writes):
            op(eng, lambda e: e.dma_start(out=out, in_=in_), reads=reads, writes=writes, dma=sem)

        def rstd_from_ssq(dst, src, nelem, reads, writes):
            act(dst, src, AF.Ln, reads, list(writes), bias=nelem * EPS)
            act(dst, dst, AF.Exp, list(writes), writes, scale=-0.5)

        def affsel(t, cmp, fill, cm, pattern):
            op('pool', lambda e: e.affine_select(out=t, in_=t, compare_op=cmp, fill=fill, base=0,
                                                 pattern=pattern, channel_multiplier=cm),
               reads=['const'], writes=['const'])

        def mset(t, v):
            op('pool', lambda e: e.memset(t, v), writes=['const'])

        mset(ident[:], 0.0)
        affsel(ident[:], ALU.not_equal, 1.0, 1, [[-1, 128]])
        mset(identf[:], 0.0)
        affsel(identf[:], ALU.not_equal, 1.0, 1, [[-1, 128]])
        mset(triF[:], -1.0 / 16)
        affsel(triF[:], ALU.is_ge, 0.0, -1, [[1, 128]])
        mset(triB[:], -1.0 / 16)
        affsel(triB[:], ALU.is_ge, 0.0, 1, [[-1, 128]])
        mset(maskF[:], 1.0)
        affsel(maskF[:], ALU.is_ge, 0.0, -1, [[0, 4], [1, 128]])
        mset(maskB[:], 1.0)
        affsel(maskB[:], ALU.is_gt, 0.0, 1, [[0, 4], [-1, 128]])
        mset(ones[:], 1.0)
        mset(e0row[:], 1.0)
        affsel(e0row[:], ALU.is_ge, 0.0, -1, [[0, 128]])
        mset(lrT[:], 0.0)
        mset(lrT[32:33, :], 1.0)
        mset(pbuf[:], 0.0)
        for b_ in range(2):
            mset(hT[b_][:], 0.0)
        mset(stat[:], 0.0)

        dma('sp', pvec[:], pvec_d, 'c0', [], ['pvec'])
        ts('dve', pvec[:, 40:48], pvec[:, 40:48], 16.0, None, ALU.mult, None, ['pvec'], ['pvec'])
        dma('sp', flags[:], flags_d, 'c1', [], ['flags'])
        dma('sp', cTs[:], cT_d, 'c2', [], ['cts0', 'cts1'])
        dma('sp', fg_bc[:], fg_d, 'c3', [], ['fg'])
        dma('pool', wg[:], wg_d.rearrange("p (a n) -> p a n", a=3), 'c4', [], ['wg'])
        dma('pool', wlr[:], wlr_d.rearrange("(kc p) n -> p kc n", p=128), 'c5', [], ['wlr'])

        ncast = [0]

        def cast(dst, src, ncols, resname):
            c0 = 0
            while c0 < ncols:
                w_ = min(1024, ncols - c0)
                dma('pool', dst[:, c0:c0 + w_], src[:, c0:c0 + w_], 'wc%d' % (ncast[0] % 4), [], [resname])
                ncast[0] += 1
                c0 += w_

        cast(wbf_ada, wada_d, 3 * D, 'wbf')
        cast(wbf_in, win_d, NIN, 'wbf')
        cast(wbf_a, wa_d, D, 'wbf')
        cast(wbf_b, wb_d, D, 'wbf')
        cast(wbf_o, wo_d, D, 'wbf')
        P.wait_all_dma('pool', ['wc0', 'wc1', 'wc2', 'wc3'])
        op('pool', lambda e: e.memset(stat[:, 63:64], 0.0), reads=[], writes=['wbf_ready'])

        wstate = dict(n=0)

        def wload(src, c0, ncols=512):
            i = wstate['n'] % NWB
            wstate['n'] += 1
            r = 'wbuf%d' % i
            dma('sp', wbuf[i][:, :, 0:ncols], src[:, c0:c0 + ncols].rearrange("(kc p) n -> p kc n", p=128),
                'wl%d' % i, ['wbf_ready'], [r])
            return wbuf[i], r

        class WStream:
            def __init__(self, units):
                self.units = units
                self.loaded = []
                self.pos = 0
                for _ in range(min(2, len(units))):
                    self._issue()

            def _issue(self):
                k = len(self.loaded)
                if k < len(self.units):
                    self.loaded.append(wload(*self.units[k]))

            def get(self):
                r = self.loaded[self.pos]
                self.pos += 1
                self._issue()
                return r

        def adaln(s, ws):
            act(cTs[:, s * 8:(s + 1) * 8], cTs[:, s * 8:(s + 1) * 8], AF.Silu, ['cts%d' % s], ['cts%d' % s])
            for kc in range(8):
                act(csb[:, kc, :], ones[:], AF.Copy, ['const', 'cts%d' % s], ['onb'],
                    scale=cTs[:, s * 8 + kc:s * 8 + kc + 1])
            for ch in range(6):
                wt, wr = ws.get()
                dma('pool', badas, bada_d[:, ch * 512:(ch + 1) * 512], 'bd', [], ['Abf'])
                bank = nxt('pj', 3)
                for kc in range(8):
                    mm(pj[bank][:], csb[:, kc, :], wt[:, kc, :], kc == 0, False, ['onb', wr], ['pj%d' % bank])
                mm(pj[bank][:], e0row[:], badas, False, True, ['const', 'Abf'], ['pj%d' % bank])
                if ch >= 4:
                    cp('dve', gate_bc[:, (ch - 4) * 512:(ch - 3) * 512], pj[bank][:], ['pj%d' % bank], ['gate_bc'])
                else:
                    cp('dve', modbc[:], pj[bank][:], ['pj%d' % bank], ['cu1'])
                    for q_ in range(4):
                        kc = (ch % 2) * 4 + q_
                        op('pe', lambda e, q_=q_: e.transpose(out=pa[1][:, q_, :], in_=modbc[:, q_ * 128:(q_ + 1) * 128],
                                                              identity=identf[:]),
                           reads=['cu1', 'const'], writes=['pa1'])
                        if ch < 2:
                            cp('dve', modv[:, s, 8 + kc:9 + kc], pa[1][:, q_, 0:1], ['pa1'], ['modv%d' % s])
                        else:
                            cp('dve', modv[:, s, kc:kc + 1], pa[1][:, q_, 0:1], ['pa1'], ['modv%d' % s])
            ts('dve', modv[:, s, 0:8], modv[:, s, 0:8], 1.0, 32.0, ALU.add, ALU.mult, ['modv%d' % s], ['modv%d' % s])
            tt('dve', modv[:, s, 0:8], modv[:, s, 0:8], pvec[:, 0:8], ALU.mult, ['modv%d' % s, 'pvec'], ['modv%d' % s])

        def prepA(src_rows):
            i = nxt('xp', 2)
            xb = xp[i]
            xr_ = 'xp%d' % i
            dma('sp', xb[:], src_rows, 'xl%d' % i, [], [xr_])
            k = nxt('st', 16)
            ssq = stat[:, k:k + 1]
            rs = stat[:, 16 + k:17 + k]
            act(junk[:], xb[:], AF.Square, [xr_], ['junk', 'ssq%d' % k], accum=ssq)
            rstd_from_ssq(rs, ssq, D, ['ssq%d' % k], ['rs%d' % k])
            n = nxt('xn', 2)
            ts('dve', xn[n][:], xb[:], rs, None, ALU.mult, None, [xr_, 'rs%d' % k], ['xn%d' % n])
            return n

        def prepB(n, hb, j, s):
            for kc in range(8):
                tr(pT[:, kc, :], xn[n][:, kc * 128:(kc + 1) * 128], ['xn%d' % n], ['pT'])
            hr = 'hT%d_%d' % (hb, j)
            for kc in range(8):
                o_ = hT[hb][:, kc, j * 128:(j + 1) * 128]
                if j % 2 == 0:
                    act(o_, pT[:, kc, :], AF.Identity, ['pT', 'modv%d' % s], [hr],
                        scale=modv[:, s, kc:kc + 1], bias=modv[:, s, 8 + kc:9 + kc])
                else:
                    ts('dve', o_, pT[:, kc, :], modv[:, s, kc:kc + 1], modv[:, s, 8 + kc:9 + kc], ALU.mult, ALU.add,
                       ['pT', 'modv%d' % s], [hr])

        def prep_sub(src_rows, hb, j, s):
            prepB(prepA(src_rows), hb, j, s)

        def prep_stages(items, s):
            ctx = {}
            n_ = len(items)
            out = []

            def mkA(i):
                def f():
                    ctx[i] = prepA(items[i][0])
                return f

            def mkB(i):
                def f():
                    prepB(ctx[i], items[i][1], items[i][2], s)
                return f

            def seq(fs):
                def f():
                    for g_ in fs:
                        g_()
                return f
            out.append(seq([mkA(i) for i in range(min(2, n_))]))
            for i in range(n_):
                fs = [mkB(i)]
                if i + 2 < n_:
                    fs.append(mkA(i + 2))
                out.append(seq(fs))
            return out

        def hreads(hb):
            return ['hT%d_%d' % (hb, j) for j in range(4)]

        def proj_fm(ws, hb, off, evac, nchunks=4, hook=None):
            wt, wr = ws.get()
            for c in range(nchunks):
                bank = nxt('pj', 3)
                rd = hreads(hb) + [wr] + (['hT%d_L' % hb] if off else [])
                for kc in range(8):
                    mm(pj[bank][:], wt[:, kc, c * 128:(c + 1) * 128], hT[hb][:, kc, off:off + 512], kc == 0, kc == 7,
                       rd, ['pj%d' % bank])
                evac(c, pj[bank], 'pj%d' % bank)
                if hook is not None:
                    hook()

        def proj_v(ws, hb, vdst=None, vres=None, hook=None):
            vdst = vsb if vdst is None else vdst
            for half in range(2):
                wt, wr = ws.get()
                for j in range(4):
                    bank = nxt('pj', 3)
                    for kc in range(8):
                        mm(pj[bank][:], hT[hb][:, kc, j * 128:(j + 1) * 128], wt[:, kc, :], kc == 0, kc == 7,
                           ['hT%d_%d' % (hb, j), wr], ['pj%d' % bank])
                    cp('act' if j % 2 == 0 else 'dve', vdst[:, j, half * 512:(half + 1) * 512], pj[bank][:],
                       ['pj%d' % bank], ['v%d' % j if vres is None else vres])
                    if hook is not None:
                        hook()

        def proj_lr(hb, ldst=None, lres='lrT'):
            ldst = lrT if ldst is None else ldst
            bank = nxt('pj', 3)
            for kc in range(8):
                mm(pj[bank][:], wlr[:, kc, :], hT[hb][:, kc, 0:512], kc == 0, kc == 7, hreads(hb) + ['wlr'],
                   ['pj%d' % bank])
            cp('dve', ldst[0:32, :], pj[bank][0:32, :], ['pj%d' % bank], [lres])

        kT2 = B[0][:, 0:4, :]
        vsb2 = B[1][:].rearrange("p (a b) c -> p a (b c)", a=4)
        lrT2 = B[2][:, 0, :]

        def state_front(ws, src, row0, s, bs, kv=None):
            hb = nxt('hb', 2)
            kd, kr = (kT, 'kT') if bs == 0 else (kT2, 'B0')
            vd, vr = (vsb, None) if bs == 0 else (vsb2, 'B1')
            ld, lr_ = (lrT, 'lrT') if bs == 0 else (lrT2, 'B2')
            fs = prep_stages([(src[row0 + j * 128:row0 + (j + 1) * 128, :], hb, j) for j in range(4)], s)

            def ev_k(c, bank, br):
                cp('act' if c % 2 == 0 else 'dve', kd[:, c, :], bank[:], [br], [kr])
            def f_k():
                proj_fm(ws, hb, 0, ev_k)
                if kv is not None:
                    ks_d, vs_d, kvname, ti_ = kv
                    dma('pool', ks_d[ti_].rearrange("p (a b) -> p a b", a=4), kd[:, :, :], 'kso', [kr],
                        ['ksd_%s_%d' % (kvname, ti_)])

            def f_v():
                proj_v(ws, hb, vd, vr)
                if kv is not None:
                    ks_d, vs_d, kvname, ti_ = kv
                    dma('pool', vs_d[ti_].rearrange("p (a b) -> p a b", a=4), vd[:, :, :], 'vso',
                        ['v0', 'v1', 'v2', 'v3'] if bs == 0 else ['B1'], ['vsd_%s_%d' % (kvname, ti_)])
            fs.append(f_k)
            fs.append(f_v)
            fs.append(lambda: proj_lr(hb, ld, lr_))
            return fs

        e0t = sb("e0t", [128, 16], F32)
        spx = [res[0][:, 0:512], res[0][:, 512:1024], res[1][:, 0:512], res[1][:, 512:1024]]
        spr = ['res0', 'res0', 'res1', 'res1']
        Eix = [xr[0][:, 0:512], xr[0][:, 512:1024], xr[1][:, 0:512], xr[1][:, 512:1024]]
        Eir = ['xr0', 'xr0', 'xr1', 'xr1']
        kTTs = [B[3][:, cc_, :].rearrange("p (a b) -> p a b", a=4) for cc_ in range(4)]
        cumb = [(pa[0][:], 'pa0'), (pa[1][:], 'pa1')]

        def state_back(bs, widx, store_d, sbname, chunk0, pre_hook=None):
            kd, kr = (kT, 'kT') if bs == 0 else (kT2, 'B0')
            vd = vsb if bs == 0 else vsb2
            ld, lr_ = (lrT, 'lrT') if bs == 0 else (lrT2, 'B2')
            if pre_hook is not None:
                pre_hook()
            order = (3, 2, 1, 0)
            for i, cc in enumerate(order):
                tk = slice(cc * 128, (cc + 1) * 128)
                pb = pa[i % 2][:].rearrange("p a b -> p (a b)")
                pr = 'pa%d' % (i % 2)
                mm(pb, ld[:, tk], wg[:, widx, :], True, True, [lr_, 'wg', 'const'], [pr])
                act(spx[cc], pb, AF.Exp, [pr], [spr[cc]], scale=-1.0)
                act(spx[cc], spx[cc], AF.Ln, [spr[cc]], [spr[cc]], bias=1.0)
                if i % 2 == 1:
                    yield
            for i, cc in enumerate(order):
                cb, cr = cumb[i % 2]
                for h in range(4):
                    mm(cb[:, h, :], spx[cc][:, h * 128:(h + 1) * 128], triB[:], True, True,
                       [spr[cc], 'const'], [cr])
                act(Eix[cc].rearrange("p (a b) -> p a b", a=4), cb, AF.Exp, [cr], [Eir[cc]],
                    scale=-1.0)
                act(e0t[:, cc * 4:(cc + 1) * 4], cb[:, :, 0], AF.Exp, [cr], ['e0t%d' % cc])
                tt('dve', qk4[cc][:], kd[:, :, cc * 128:(cc + 1) * 128], Eix[cc].rearrange("p (a b) -> p a b", a=4),
                   ALU.mult, [kr, Eir[cc]], ['qk%d' % cc])
                yield
            for pair in ((3, 2), (1, 0)):
                for q_, cc in enumerate(pair):
                    for h in range(4):
                        tr(pT[:, q_ * 4 + h, :], qk4[cc][:, h, :], ['qk%d' % cc], ['pT'])
                for q_, cc in enumerate(pair):
                    cp('act' if q_ == 0 else 'dve', kTTs[cc], pT[:, q_ * 4:(q_ + 1) * 4, :], ['pT'], ['B3'])
                yield
            cur, cur_r, nx_, nx_r = Sb, 'Sb', tmpS, 'tmpS'
            for cc in order:
                vr = ('v%d' % cc) if bs == 0 else 'B1'
                if store_d is not None:
                    sl = nxt('sbs', 2)
                    cp('act', Sbbf[sl][:], cur[:], [cur_r], ['Sbbf%d' % sl])
                    dma('pool', store_d[chunk0 + cc], Sbbf[sl][:], 'sbo%d' % sl, ['Sbbf%d' % sl],
                        ['sbd_%s_%d' % (sbname, chunk0 + cc)])
                for h in range(4):
                    mm(po[:, h * 256:(h + 1) * 256], kTTs[cc][:, h, :], vd[:, cc, h * 256:(h + 1) * 256], True, True,
                       ['B3', vr], ['po'])
                for h in range(4):
                    ts('dve', nx_[:, h * 256:(h + 1) * 256], cur[:, h * 256:(h + 1) * 256],
                       e0t[:, cc * 4 + h:cc * 4 + h + 1], None, ALU.mult, None, [cur_r, 'e0t%d' % cc], [nx_r])
                for h in range(4):
                    stt('dve', nx_[:, h * 256:(h + 1) * 256], po[:, h * 256:(h + 1) * 256],
                        e0t[:, cc * 4 + h:cc * 4 + h + 1], nx_[:, h * 256:(h + 1) * 256], ALU.mult, ALU.add,
                        ['po', 'e0t%d' % cc, nx_r], [nx_r])
                cur, cur_r, nx_, nx_r = nx_, nx_r, cur, cur_r
                yield

        def zip_run(fronts, back, nback):
            nf = len(fronts)
            done = 0
            alive = back is not None
            for i, f in enumerate(fronts):
                f()
                want = ((i + 1) * nback + nf - 1) // nf if nf else nback
                while alive and done < want:
                    try:
                        next(back)
                        done += 1
                    except StopIteration:
                        alive = False
            if alive:
                for _ in back:
                    pass

        def state_sweep(ws, s, tiles):
            op('pool', lambda e: e.memset(lrT2, 0.0), writes=['B2'])
            op('pool', lambda e: e.memset(B[2][32:33, 0, :], 1.0), writes=['B2'])
            n = len(tiles)
            if n == 0:
                return
            hbs = []

            def mk_prep(i):
                t = tiles[i]
                hb = nxt('hb', 2)
                hbs.append(hb)
                return prep_stages([(t['src'][t['row0'] + j * 128:t['row0'] + (j + 1) * 128, :], hb, j)
                                    for j in range(4)], s)
            for f in mk_prep(0):
                f()
            prev = None
            for i, t in enumerate(tiles):
                bs = i % 2
                hb = hbs[i]
                kd, kr = (kT, 'kT') if bs == 0 else (kT2, 'B0')
                vd, vr = (vsb, None) if bs == 0 else (vsb2, 'B1')
                ld, lr_ = (lrT, 'lrT') if bs == 0 else (lrT2, 'B2')
                stages = []
                pst = mk_prep(i + 1) if i + 1 < n else []
                bst = []
                if prev is not None:
                    g_ = prev
                    bst = [(lambda g_=g_: next(g_, None)) for _ in range(12)]
                nb, npst = len(bst), len(pst)
                tot = nb + npst
                ib = ip = 0
                for k_ in range(tot):
                    if ip < npst and (ib >= nb or ip * tot <= k_ * npst):
                        stages.append(pst[ip])
                        ip += 1
                    else:
                        stages.append(bst[ib])
                        ib += 1
                st = dict(slot=0, done=0)
                NSL = 12

                def tick(st=st, stages=stages):
                    st['slot'] += 1
                    want = min(len(stages), (st['slot'] * len(stages) + NSL - 1) // NSL)
                    while st['done'] < want:
                        stages[st['done']]()
                        st['done'] += 1

                def ev_k(c, bank, br, kd=kd, kr=kr):
                    cp('act' if c % 2 == 0 else 'dve', kd[:, c, :], bank[:], [br], [kr])
                proj_fm(ws, hb, 0, ev_k, hook=tick)
                kv = t.get('kv')
                if kv is not None:
                    ks_d, vs_d, kvname, ti_ = kv
                    dma('pool', ks_d[ti_].rearrange("p (a b) -> p a b", a=4), kd[:, :, :], 'kso', [kr],
                        ['ksd_%s_%d' % (kvname, ti_)])
                proj_v(ws, hb, vd, vr, hook=tick)
                if kv is not None:
                    dma('pool', vs_d[ti_].rearrange("p (a b) -> p a b", a=4), vd[:, :, :], 'vso',
                        ['v0', 'v1', 'v2', 'v3'] if bs == 0 else ['B1'], ['vsd_%s_%d' % (kvname, ti_)])
                proj_lr(hb, ld, lr_)
                while st['done'] < len(stages):
                    stages[st['done']]()
                    st['done'] += 1
                prev = state_back(bs, t['widx'], t['store_d'], t['sbname'], t['chunk0'], t.get('pre_hook'))
            if prev is not None:
                for _ in prev:
                    pass

        def main_tile(ws, src, row0, s, ti, ntiles, hb, sb_d, sbname, chunk0, ydst, yrow0, flagR_col, side=(), kv=None):
            def fm_unit(evac, off=0):
                return lambda: proj_fm(ws, hb, off, evac)

            def fmB_unit(Bt, Br, evac):
                return lambda: proj_fm_B(ws, Bt, Br, evac)

            def gla_stages(cc):
                tk = slice(cc * 128, (cc + 1) * 128)
                ctx = {}
                g1, g2 = 0, 1

                def y1():
                    pf = pa[0][:].rearrange("p a b -> p (a b)")
                    pb_ = pa[1][:].rearrange("p a b -> p (a b)")
                    mm(pf, lrT[:, tk], wg[:, 0, :], True, True, ['lrT', 'wg', 'const'], ['pa0'])
                    mm(pb_, lrT[:, tk], wg[:, 1, :], True, True, ['lrT', 'wg', 'const'], ['pa1'])
                    act(cu[g1][:], pf, AF.Exp, ['pa0'], ['cu%d' % g1], scale=-1.0)
                    act(cu[g1][:], cu[g1][:], AF.Ln, ['cu%d' % g1], ['cu%d' % g1], bias=1.0)
                    act(cu[g2][:], pb_, AF.Exp, ['pa1'], ['cu%d' % g2], scale=-1.0)
                    act(cu[g2][:], cu[g2][:], AF.Ln, ['cu%d' % g2], ['cu%d' % g2], bias=1.0)

                def y2():
                    for h in range(4):
                        mm(pa[0][:, h, :], cu[g1][:, h * 128:(h + 1) * 128], triF[:], True, True,
                           ['cu%d' % g1, 'const'], ['pa0'])
                    act(E4[0][:], pa[0][:], AF.Exp, ['pa0'], ['E0'])
                    act(E4[1][:], pa[0][:], AF.Exp, ['pa0'], ['E1'], scale=-1.0)
                    act(e0t[:, cc * 4:(cc + 1) * 4], pa[0][:, :, 127], AF.Exp, ['pa0'], ['e0t%d' % cc])
                    tt('dve', qk4[0][:], qT[:, :, tk], E4[0][:], ALU.mult, ['qT', 'E0'], ['qk0'])
                    tt('dve', qk4[1][:], kT[:, :, tk], E4[1][:], ALU.mult, ['kT', 'E1'], ['qk1'])

                def y3():
                    for h in range(4):
                        mm(pa[1][:, h, :], cu[g2][:, h * 128:(h + 1) * 128], triB[:], True, True,
                           ['cu%d' % g2, 'const'], ['pa1'])
                    act(E4[2][:], pa[1][:], AF.Exp, ['pa1'], ['E2'])
                    act(E4[3][:], pa[1][:], AF.Exp, ['pa1'], ['E3'], scale=-1.0)
                    tt('dve', qk4[2][:], qT[:, :, tk], E4[2][:], ALU.mult, ['qT', 'E2'], ['qk2'])
                    tt('dve', qk4[3][:], kT[:, :, tk], E4[3][:], ALU.mult, ['kT', 'E3'], ['qk3'])
                    sl = nxt('sbs', 2)
                    ctx['sl'] = sl
                    dma('sp', Sbbf[sl][:], sb_d[chunk0 + cc], 'sbi%d' % sl,
                        ['sbd_%s_%d' % (sbname, chunk0 + cc)], ['Sbbf%d' % sl])

                def y4a():
                    for h in range(4):
                        mm(pa[0][:, h, :], qk4[1][:, h, :], qk4[0][:, h, :], True, True, ['qk0', 'qk1'], ['pa0'])
                    for h in range(4):
                        mm(pa[1][:, h, :], qk4[3][:, h, :], qk4[2][:, h, :], True, True, ['qk2', 'qk3'], ['pa1'])
                    tt('dve', tmpA[:], pa[0][:], maskF[:], ALU.mult, ['pa0', 'const'], ['tmpA'])
                    tt('dve', Abf[:], pa[1][:], maskB[:], ALU.mult, ['pa1', 'const'], ['Abf'])
                    tt('dve', Abf[:], Abf[:], tmpA[:], ALU.add, ['Abf', 'tmpA'], ['Abf'])

                def y4b():
                    for h in range(4):
                        tr(pT[:, h, :], qk4[1][:, h, :], ['qk1'], ['pT'])
                    cp('act', kTT[:], pT[:, 0:4, :], ['pT'], ['kTT'])

                def y5():
                    sl = ctx['sl']
                    for h in range(4):
                        o_ = po[:, h * 256:(h + 1) * 256]
                        mm(o_, Abf[:, h, :], vsb[:, cc, h * 256:(h + 1) * 256], True, False, ['Abf', 'v%d' % cc], ['po'])
                        mm(o_, qk4[0][:, h, :], Sfbf[:, h * 256:(h + 1) * 256], False, False, ['qk0', 'Sfbf'], ['po'])
                        mm(o_, qk4[2][:, h, :], Sbbf[sl][:, h * 256:(h + 1) * 256], False, True,
                           ['qk2', 'Sbbf%d' % sl], ['po'])
                    k = nxt('st4', 4)
                    ssq4 = stat[:, 32 + 4 * k:36 + 4 * k]
                    rs4 = stat[:, 48 + 4 * k:52 + 4 * k]
                    for h in range(4):
                        act(junk[:, h * 256:(h + 1) * 256], po[:, h * 256:(h + 1) * 256], AF.Square, ['po'],
                            ['junk', 'ssq4_%d' % k], accum=ssq4[:, h:h + 1])
                    rstd_from_ssq(rs4, ssq4, 256, ['ssq4_%d' % k], ['rs4_%d' % k])
                    for h in range(4):
                        ts('dve', onb[:, h * 256:(h + 1) * 256], po[:, h * 256:(h + 1) * 256], rs4[:, h:h + 1], None,
                           ALU.mult, None, ['po', 'rs4_%d' % k], ['onb'])

                def y6():
                    for kc in range(8):
                        tr(pT[:, kc, :], onb[:, kc * 128:(kc + 1) * 128], ['onb'], ['pT'])
                    for h in range(4):
                        mm(po[:, h * 256:(h + 1) * 256], kTT[:, h, :], vsb[:, cc, h * 256:(h + 1) * 256], True, True,
                           ['kTT', 'v%d' % cc], ['po'])
                    for kc in range(8):
                        stt('dve', B[2][:, kc, tk], pT[:, kc, :], pvec[:, 40 + kc:41 + kc], B[0][:, kc, tk], ALU.mult,
                            ALU.mult, ['pT', 'pvec', 'B0'], ['B2'])
                    tt('dve', tmpS[:], po[:], Sf[:], ALU.add, ['po', 'Sf'], ['tmpS'])
                    for h in range(4):
                        ts('dve', Sfbf[:, h * 256:(h + 1) * 256], tmpS[:, h * 256:(h + 1) * 256],
                           e0t[:, cc * 4 + h:cc * 4 + h + 1], None, ALU.mult, None, ['tmpS', 'e0t%d' % cc], ['Sfbf'])
                    for h in range(4):
                        act(Sf[:, h * 256:(h + 1) * 256], tmpS[:, h * 256:(h + 1) * 256], AF.Copy,
                            ['tmpS', 'e0t%d' % cc], ['Sf'], scale=e0t[:, cc * 4 + h:cc * 4 + h + 1])
                return [y1, y2, y3, y4a, y4b, y5, y6]

            def gla_gen():
                st = [gla_stages(cc) for cc in range(4)]
                seq = st[0][0:6]
                for c in range(4):
                    if c + 1 < 4:
                        seq.append(st[c + 1][0])
                    seq.append(st[c][6])
                    if c + 1 < 4:
                        seq += st[c + 1][1:6]
                assert len(seq) == 28
                for f in seq:
                    f()
                    yield

            def ev_q(c, bank, br):
                act(qT[:, c, :], bank[:], AF.Copy, [br], ['qT'], scale=128 ** -0.5)

            def ev_k(c, bank, br):
                cp('dve', kT[:, c, :], bank[:], [br], ['kT'])

            def mk(kind, half):
                def ev(c, bank, br):
                    cc_ = half * 4 + c
                    if kind == 'bz':
                        act(B[0][:, cc_, :], bank[:], AF.Silu, [br], ['B0'])
                    elif kind == 'az':
                        act(B[1][:, cc_, :], bank[:], AF.Silu, [br], ['B1'])
                    elif kind == 'ab':
                        tt('dve', B[1][:, cc_, :], bank[:], B[1][:, cc_, :], ALU.mult, [br, 'B1'], ['B1'])
                    elif kind == 'ac':
                        act(B[3][:, cc_, :], bank[:], AF.Copy, [br], ['B3'])
                    elif kind == 'ax':
                        tt('dve', pbuf[:, cc_, 2:514], bank[:], B[3][:, cc_, :], ALU.mult, [br, 'B3'],
                           ['pbuf'])
                        if CONV_INLINE:
                            if ti == ntiles - 1:
                                ts('dve', pbuf[:, cc_, 513:514], pbuf[:, cc_, 513:514], flags[:, flagR_col:flagR_col + 1],
                                   None, ALU.mult, None, ['pbuf', 'flags'], ['pbuf'])
                            a0 = nxt('ct', 3)
                            act(ct[a0][:], pbuf[:, cc_, 1:513], AF.Identity, ['pbuf', 'pvec'], ['ct%d' % a0],
                                scale=pvec[:, 16 + cc_:17 + cc_], bias=pvec[:, 32 + cc_:33 + cc_])
                            stt('dve', ct[a0][:], pbuf[:, cc_, 0:512], pvec[:, 8 + cc_:9 + cc_], ct[a0][:], ALU.mult, ALU.add,
                                ['pbuf', 'pvec', 'ct%d' % a0], ['ct%d' % a0])
                            stt('dve', ct[a0][:], pbuf[:, cc_, 2:514], pvec[:, 24 + cc_:25 + cc_], ct[a0][:], ALU.mult,
                                ALU.add, ['pbuf', 'pvec', 'ct%d' % a0], ['ct%d' % a0])
                            tt('pool', B[3][:, cc_, :], ct[a0][:], B[1][:, cc_, :], ALU.mult, ['ct%d' % a0, 'B1'],
                               ['B3'])
                            cp('pool', pbuf[:, cc_, 0:2], pbuf[:, cc_, 512:514], ['pbuf'], ['pbuf'])
                    elif kind == 'ma':
                        act(B[1][:, cc_, :], bank[:], AF.Sigmoid, [br], ['B1'])
                    elif kind == 'wa':
                        tt('dve', B[1][:, cc_, :], bank[:], B[1][:, cc_, :], ALU.mult, [br, 'B1'], ['B1'])
                    elif kind == 'mb':
                        act(B[3][:, cc_, :], bank[:], AF.Sigmoid, [br], ['B3'])
                    elif kind == 'wb':
                        tt('dve', tmpb[:], bank[:], B[3][:, cc_, :], ALU.mult, [br, 'B3'], ['cu0'])
                        tt('dve', B[1][:, cc_, :], B[1][:, cc_, :], tmpb[:], ALU.add, ['B1', 'cu0'], ['B1'])
                return ev


            def conv_unit():
                PB = ['pbuf']
                if ti == ntiles - 1:
                    ts('dve', pbuf[:, :, 513:514], pbuf[:, :, 513:514], flags[:, flagR_col:flagR_col + 1], None,
                       ALU.mult, None, PB + ['flags'], PB)
                for c in range(8):
                    a0 = nxt('ct', 3)
                    act(ct[a0][:], pbuf[:, c, 1:513], AF.Identity, PB + ['pvec'], ['ct%d' % a0],
                        scale=pvec[:, 16 + c:17 + c], bias=pvec[:, 32 + c:33 + c])
                    stt('dve', ct[a0][:], pbuf[:, c, 0:512], pvec[:, 8 + c:9 + c], ct[a0][:], ALU.mult, ALU.add,
                        PB + ['pvec', 'ct%d' % a0], ['ct%d' % a0])
                    stt('dve', ct[a0][:], pbuf[:, c, 2:514], pvec[:, 24 + c:25 + c], ct[a0][:], ALU.mult, ALU.add,
                        PB + ['pvec', 'ct%d' % a0], ['ct%d' % a0])
                    tt('dve', B[3][:, c, :], ct[a0][:], B[1][:, c, :], ALU.mult, ['ct%d' % a0, 'B1'], ['B3'])
                    tick()
                cp('pool', pbuf[:, :, 0:2], pbuf[:, :, 512:514], PB, PB)

            B3all = ['B3']
            g = gla_gen()
            NST = 28
            state = dict(done=0, slot=0, alive=True)
            NSLOT = 54

            def tick():
                state['slot'] += 1
                want = (state['slot'] * NST + NSLOT - 1) // NSLOT
                while state['alive'] and state['done'] < want:
                    try:
                        next(g)
                        state['done'] += 1
                    except StopIteration:
                        state['alive'] = False

            def fmu(kind, half, off=0):
                return lambda: proj_fm(ws, hb, off, mk(kind, half), hook=tick)

            ks_d, vs_d, kvname = kv
            dma('sp', kT[:], ks_d[ti].rearrange("p (a b) -> p a b", a=4), 'ksi', ['ksd_%s_%d' % (kvname, ti)], ['kT'])
            dma('sp', vsb[:], vs_d[ti].rearrange("p (a b) -> p a b", a=4), 'vsi', ['vsd_%s_%d' % (kvname, ti)],
                ['v0', 'v1', 'v2', 'v3'])
            head = [lambda: proj_fm(ws, hb, 0, ev_q),
                    lambda: proj_lr(hb), lambda: proj_fm(ws, hb, 0, mk('bz', 0)), lambda: proj_fm(ws, hb, 0, mk('bz', 1))]
            mid = [fmu('az', 0), fmu('az', 1), fmu('ab', 0), fmu('ab', 1), fmu('ac', 0, 1), fmu('ac', 1, 1),
                   fmu('ax', 0, 1), fmu('ax', 1, 1)] + ([] if CONV_INLINE else [conv_unit]) + [fmu('ma', 0), fmu('ma', 1),
                   lambda: proj_fm_B(ws, B[3], B3all, mk('wa', 0), hook=tick),
                   lambda: proj_fm_B(ws, B[3], B3all, mk('wa', 1), hook=tick), fmu('mb', 0), fmu('mb', 1)]
            side = list(side)
            for u in head + mid:
                if side:
                    side.pop(0)()
                u()
            while side:
                side.pop(0)()
            for _ in g:
                pass
            proj_fm_B(ws, B[2], 'B2', mk('wb', 0))
            proj_fm_B(ws, B[2], 'B2', mk('wb', 1))
            wts = [ws.get(), ws.get()]
            deferred = None
            for j in range(4):
                ri = nxt('res', 2)
                xi = nxt('xr', 2)
                dma('pool', xr[xi][:], src[row0 + j * 128:row0 + (j + 1) * 128, :], 'xrl%d' % xi, [], ['xr%d' % xi])
                for half in range(2):
                    wt, wr = wts[half]
                    bank = nxt('pj', 3)
                    for kc in range(8):
                        mm(pj[bank][:], B[1][:, kc, j * 128:(j + 1) * 128], wt[:, kc, :], kc == 0, kc == 7, ['B1', wr],
                           ['pj%d' % bank])
                    hs = slice(half * 512, (half + 1) * 512)
                    tt('dve', res[ri][:, hs], pj[bank][:], gate_bc[:, hs], ALU.mult, ['pj%d' % bank, 'gate_bc'],
                       ['res%d' % ri])
                tt('dve', res[ri][:], res[ri][:], xr[xi][:], ALU.add, ['res%d' % ri, 'xr%d' % xi], ['res%d' % ri])
                k = nxt('st', 16)
                ssq = stat[:, k:k + 1]
                rs = stat[:, 16 + k:17 + k]
                act(junk[:], res[ri][:], AF.Square, ['res%d' % ri], ['junk', 'ssq%d' % k], accum=ssq)
                act(rs, ssq, AF.Ln, ['ssq%d' % k], ['rs%d' % k], scale=1.0 / D, bias=EPS)
                act(rs, rs, AF.Exp, ['rs%d' % k], ['rs%d' % k], scale=-0.5)
                if deferred is not None:
                    deferred()

                def fin(ri=ri, k=k, rs=rs, j=j):
                    stt('dve', res[ri][:], res[ri][:], rs, fg_bc[:], ALU.mult, ALU.mult,
                        ['res%d' % ri, 'rs%d' % k, 'fg'], ['res%d' % ri])
                    dma('pool', ydst[yrow0 + j * 128:yrow0 + (j + 1) * 128, :], res[ri][:], 'yo%d' % ri,
                        ['res%d' % ri], [])
                deferred = fin
            return deferred

        def proj_fm_cols(ws, hb, evac):
            proj_fm(ws, hb, 0, evac)

        def proj_fm_B(ws, Bt, Br, evac, hook=None):
            wt, wr = ws.get()
            for c in range(4):
                bank = nxt('pj', 3)
                for kc in range(8):
                    mm(pj[bank][:], wt[:, kc, c * 128:(c + 1) * 128], Bt[:, kc, :], kc == 0, kc == 7,
                       (Br if isinstance(Br, list) else [Br]) + [wr], ['pj%d' % bank])
                evac(c, pj[bank], 'pj%d' % bank)
                if hook is not None:
                    hook()

        def units_state():
            return [(wbf_in, C_K), (wbf_in, C_V), (wbf_in, C_V + 512)]

        def units_pre():
            return [(wbf_in, C_AC), (wbf_in, C_AC + 512), (wbf_in, C_AX), (wbf_in, C_AX + 512)]

        def units_main():
            u = [(wbf_in, C_Q)]
            for base in (C_BZ, C_AZ, C_AB, C_AC, C_AX, C_MA):
                u += [(wbf_in, base), (wbf_in, base + 512)]
            u += [(wbf_a, 0), (wbf_a, 512)]
            u += [(wbf_in, C_MB), (wbf_in, C_MB + 512)]
            u += [(wbf_b, 0), (wbf_b, 512), (wbf_o, 0), (wbf_o, 512)]
            return u

        def units_ada():
            return [(wbf_ada, ch * 512) for ch in range(6)]

        def segment(s, xsrc, n_own_tiles, warm_src, n_warm_tiles, sb_d, sbname, ydst, flagL_col, flagR_col, use_m, kvd):
            units = units_ada()
            units += units_state() * (n_warm_tiles + n_own_tiles)
            units += units_pre()
            units += units_main() * n_own_tiles
            ws = WStream(units)
            adaln(s, ws)
            op('pool', lambda e: e.memset(Sb[:], 0.0), writes=['Sb'])
            op('pool', lambda e: e.memset(Sf[:], 0.0), writes=['Sf'])
            tiles = []
            for t in range(n_warm_tiles - 1, -1, -1):
                tiles.append(dict(src=warm_src, row0=t * 512, widx=2, store_d=None, sbname=None, chunk0=0))

            def m_hook():
                ts('dve', Sf[:], Sb[:], flags[:, 3:4], None, ALU.mult, None, ['Sb', 'flags'], ['Sf'])
                ts('dve', Sb[:], Sb[:], flags[:, 0:1], None, ALU.mult, None, ['Sb', 'flags'], ['Sb'])
            for i_, t in enumerate(range(n_own_tiles - 1, -1, -1)):
                d_ = dict(src=xsrc, row0=128 + t * 512, widx=1, store_d=sb_d, sbname=sbname, chunk0=t * 4,
                          kv=(kvd[0], kvd[1], kvd[2], t))
                if use_m and i_ == 0:
                    d_['pre_hook'] = m_hook
                tiles.append(d_)
            state_sweep(ws, s, tiles)
            cp('pool', Sfbf[:], Sf[:], ['Sf'], ['Sfbf'])
            hb0 = nxt('hb', 2)
            prep_sub(xsrc[0:128, :], hb0, 3, s)
            hb1 = nxt('hb', 2)
            for j in range(4):
                prep_sub(xsrc[128 + j * 128:128 + (j + 1) * 128, :], hb1, j, s)
            cp('pool', hT[hb0][:, :, 512:513], hT[hb1][:, :, 0:1], ['hT%d_0' % hb1], ['hT%d_L' % hb0])
            for half in range(2):
                def ev_ac(c, bank, br, half=half):
                    act(B[3][:, half * 4 + c, :], bank[:], AF.Copy, [br], ['B3'])
                proj_fm(ws, hb0, 1, ev_ac)
            for half in range(2):
                def ev_ax(c, bank, br, half=half):
                    tt('dve', pbuf[:, half * 4 + c, 2:514], bank[:], B[3][:, half * 4 + c, :], ALU.mult, [br, 'B3'],
                       ['pbuf'])
                proj_fm(ws, hb0, 1, ev_ax)
            ts('pool', pbuf[:, :, 512:513], pbuf[:, :, 512:513], flags[:, flagL_col:flagL_col + 1], None, ALU.mult,
               None, ['pbuf'] + ['flags'], ['pbuf'])
            cp('pool', pbuf[:, :, 0:2], pbuf[:, :, 512:514], ['pbuf'], ['pbuf'])
            hb = hb1
            pending = None
            for t in range(n_own_tiles):
                hbn = nxt('hb', 2)
                assert hbn != hb
                if t + 1 < n_own_tiles:
                    items = [(xsrc[128 + (t + 1) * 512 + j * 128:128 + (t + 1) * 512 + (j + 1) * 128, :], hbn, j)
                             for j in range(4)]
                else:
                    r0 = 128 + (t + 1) * 512
                    items = [(xsrc[r0:r0 + 128, :], hbn, 0)]
                side = prep_stages(items, s)

                def la(hb=hb, hbn=hbn):
                    cp('pool', hT[hb][:, :, 512:513], hT[hbn][:, :, 0:1], ['hT%d_0' % hbn], ['hT%d_L' % hb])
                side.insert(2, la)
                if pending is not None:
                    side.insert(1, pending)
                pending = main_tile(ws, xsrc, 128 + t * 512, s, t, n_own_tiles, hb, sb_d, sbname, t * 4, ydst, t * 512,
                                    flagR_col, side, kvd)
                hb = hbn
            if pending is not None:
                pending()

        segment(0, xw, NTW, None, 0, sbw_d, 'sbw', yw, 4, 4, False, (ksw_d, vsw_d, 'w'))
        segment(1, xh, NTH, xo, NTH, sbh_d, 'sbh', yh, 1, 2, True, (ksh_d, vsh_d, 'h'))
        P.wait_all_dma('pool', ['yo0', 'yo1'])
        P.emit()
    return nc


_NC_CACHE = {}


def _fm(v):
    return np.ascontiguousarray(np.asarray(v, np.float32).reshape(8, 128).T)


def kernel(**inputs):
    f32 = lambda a: np.ascontiguousarray(np.asarray(a, dtype=np.float32))
    xp_, xs_ = f32(inputs['x_prompt']), f32(inputs['x_sample'])
    cp_, cs_ = f32(inputs['c_prompt']), f32(inputs['c_sample'])
    nP, L, _ = xp_.shape
    nS = xs_.shape[0]
    seqs = [(0, i) for i in range(nP)] + [(1, i) for i in range(nS)]
    nseq = len(seqs)
    assert nseq % 3 == 0
    ncores = 2 * (nseq // 3)
    HL = L // 2

    def X(k):
        g, i = seqs[k]
        return (xp_ if g == 0 else xs_)[i]

    def Cv(k):
        g, i = seqs[k]
        return (cp_ if g == 0 else cs_)[i]

    w_in = f32(inputs['w_in'][0])
    w_ada = f32(inputs['w_ada'][0])
    w_a = f32(inputs['w_a_out'][0])
    w_b = f32(inputs['w_b_out'][0])
    w_o = f32(inputs['w_out'][0])
    conv_w = f32(inputs['conv_w'][0])
    pvec = np.concatenate([_fm(inputs['norm_g'][0]), _fm(conv_w[0]), _fm(conv_w[1]), _fm(conv_w[2]),
                           _fm(inputs['conv_b'][0]), _fm(inputs['gla_norm_g'][0])], axis=1)
    pvec = np.ascontiguousarray(pvec, dtype=np.float32)
    bada = np.zeros((128, 3 * D), np.float32)
    bada[0] = f32(inputs['b_ada'][0])
    wgf = np.zeros((128, 512), np.float32)
    wgf[0:16] = f32(inputs['w_gate_f'][0])
    wgf[32] = f32(inputs['b_gate_f'][0])
    wgb = np.zeros((128, 512), np.float32)
    wgb[16:32] = f32(inputs['w_gate_b'][0])
    wgb[32] = f32(inputs['b_gate_b'][0])
    wlr = np.zeros((D, 128), np.float32)
    wlr[:, 0:32] = w_in[:, C_LR:C_LR + 32]
    fg = np.ascontiguousarray(np.broadcast_to(f32(inputs['final_norm_g'])[None, :], (128, D)))
    zrows = np.zeros((128, D), np.float32)

    in_maps = []
    for core in range(ncores):
        j, odd = core // 2, core % 2
        kw = 3 * j + (2 if odd else 0)
        kb = 3 * j + 1
        xb = X(kb)
        xw_ = np.concatenate([zrows, X(kw), zrows], axis=0)
        if not odd:
            post = zrows.copy()
            post[0] = xb[HL]
            xh_ = np.concatenate([zrows, xb[:HL], post], axis=0)
            xo_ = np.ascontiguousarray(xb[HL:])
            fl = [1.0, 0.0, 1.0, 0.0]
            wgw = wgb
        else:
            pre = zrows.copy()
            pre[127] = xb[HL - 1]
            xh_ = np.concatenate([pre, xb[HL:], zrows], axis=0)
            xo_ = np.ascontiguousarray(xb[:HL][::-1])
            fl = [0.0, 1.0, 0.0, 1.0]
            wgw = wgf
        flags = np.zeros((128, 8), np.float32)
        flags[:, 0:4] = np.asarray(fl, np.float32)[None, :]
        cT = np.concatenate([_fm(Cv(kw)), _fm(Cv(kb))], axis=1)
        wg = np.ascontiguousarray(np.stack([wgf, wgb, wgw], axis=1).reshape(128, 3 * 512))
        in_maps.append(dict(xw=np.ascontiguousarray(xw_), xh=np.ascontiguousarray(xh_), xo=xo_,
                            cT=np.ascontiguousarray(cT, dtype=np.float32), flags=flags, pvec=pvec, bada=bada,
                            w_ada=w_ada, w_in=w_in, w_a_out=w_a, w_b_out=w_b, w_out=w_o, wg=wg, wlr=wlr, fg=fg))
    if L not in _NC_CACHE:
        _NC_CACHE[L] = build(L)
    nc = _NC_CACHE[L]
    res = run_bass_kernel_spmd(nc, in_maps, core_ids=list(range(ncores)))
    outs = [None] * nseq
    for core in range(ncores):
        j, odd = core // 2, core % 2
        r = res.results[core]
        kw = 3 * j + (2 if odd else 0)
        kb = 3 * j + 1
        outs[kw] = np.asarray(r['yw'], np.float32)
        if outs[kb] is None:
            outs[kb] = np.zeros((L, D), np.float32)
        if not odd:
            outs[kb][:HL] = np.asarray(r['yh'], np.float32)
        else:
            outs[kb][HL:] = np.asarray(r['yh'], np.float32)
    y_prompt = np.stack(outs[:nP], axis=0)
    y_sample = np.stack(outs[nP:], axis=0)
    return (y_prompt, y_sample)
```

```python
import numpy as np
from contextlib import ExitStack
import concourse.bass as bass
import concourse.mybir as mybir
from concourse.bass_utils import run_bass_kernel_spmd

F32 = mybir.dt.float32
BF16 = mybir.dt.bfloat16
AF = mybir.ActivationFunctionType
ALU = mybir.AluOpType

import os
CONV_INLINE = 0
D = 1024
NIN = 9248
EPS = 1e-6
ENGS = ['pe', 'act', 'dve', 'pool', 'sp']
C_AB, C_AC, C_AX, C_AZ, C_Q, C_K, C_V, C_BZ, C_LR, C_MA, C_MB = (
    0, 1024, 2048, 3072, 4096, 4608, 5120, 6144, 7168, 7200, 8224)


class Prog:
    def __init__(self, nc, stack):
        self.nc = nc
        self.stack = stack
        self.ops = {e: [] for e in ENGS}
        self.last_w = {}
        self.readers = {}
        self.dsem_total = {}
        self.dsems = {}
        self.waited_c = {e: {p: -1 for p in ENGS} for e in ENGS}
        self.waited_d = {e: {} for e in ENGS}
        self.csem = {e: stack.enter_context(nc.semaphore('cs_' + e)) for e in ENGS}

    def dsem(self, name):
        if name not in self.dsems:
            self.dsems[name] = self.stack.enter_context(self.nc.semaphore('ds_' + name))
            self.dsem_total[name] = 0
        return name

    def _collect(self, eng, tok, cand):
        if tok is None:
            return
        if tok[0] == 'c':
            _, peng, idx = tok
            if peng == eng and eng == 'pe':
                return
            key = ('c', peng)
            if cand.get(key, -1) < idx:
                cand[key] = idx
        else:
            _, sname, val = tok
            key = ('d', sname)
            if cand.get(key, 0) < val:
                cand[key] = val

    def _flush(self, eng, cand, waits):
        for key, v in cand.items():
            if key[0] == 'c':
                peng = key[1]
                if self.waited_c[eng][peng] >= v:
                    continue
                self.waited_c[eng][peng] = v
                self.ops[peng][v]['flag'] = True
                waits.append(('c', peng, v))
            else:
                sname = key[1]
                if self.waited_d[eng].get(sname, 0) >= v:
                    continue
                self.waited_d[eng][sname] = v
                waits.append(('d', sname, v))

    def op(self, eng, fn, reads=(), writes=(), dma=None):
        cand = {}
        for r in reads:
            self._collect(eng, self.last_w.get(r), cand)
        for w in writes:
            self._collect(eng, self.last_w.get(w), cand)
            for t in self.readers.get(w, []):
                self._collect(eng, t, cand)
        idx = len(self.ops[eng])
        rec = dict(fn=fn, waits=None, flag=False, dma=None)
        if dma is not None:
            self.dsem(dma)
            prev = self.dsem_total[dma]
            if prev > 0:
                self._collect(eng, ('d', dma, prev), cand)
            self.dsem_total[dma] = prev + 16
            tok = ('d', dma, prev + 16)
            rec['dma'] = dma
        else:
            tok = ('c', eng, idx)
        waits = []
        self._flush(eng, cand, waits)
        rec['waits'] = waits
        self.ops[eng].append(rec)
        for r in reads:
            self.readers.setdefault(r, []).append(tok)
        for w in writes:
            self.last_w[w] = tok
            self.readers[w] = []
        return tok

    def wait_all_dma(self, eng, names):
        waits = []
        cand = {}
        for n in names:
            if n in self.dsem_total and self.dsem_total[n] > 0:
                self._collect(eng, ('d', n, self.dsem_total[n]), cand)
        self._flush(eng, cand, waits)
        self.ops[eng].append(dict(fn=None, waits=waits, flag=False, dma=None))

    def emit(self):
        nc = self.nc
        cum = {}
        for e in ENGS:
            c = 0
            arr = []
            for o in self.ops[e]:
                if o['flag']:
                    c += 1
                arr.append(c)
            cum[e] = arr

        def run(eng_name, eng):
            for o in self.ops[eng_name]:
                for w in o['waits']:
                    if w[0] == 'c':
                        eng.wait_ge(self.csem[w[1]], cum[w[1]][w[2]])
                    else:
                        eng.wait_ge(self.dsems[w[1]], w[2])
                if o['fn'] is None:
                    continue
                ins = o['fn'](eng)
                if o['dma'] is not None:
                    ins.then_inc(self.dsems[o['dma']], 16)
                elif o['flag']:
                    ins.then_inc(self.csem[eng_name], 1)

        with nc.Block() as block:
            @block.tensor
            def _(e):
                run('pe', e)

            @block.scalar
            def _(e):
                run('act', e)

            @block.vector
            def _(e):
                run('dve', e)

            @block.gpsimd
            def _(e):
                run('pool', e)

            @block.sync
            def _(e):
                run('sp', e)


def build(L):
    HL = L // 2
    NTW = L // 512
    NTH = HL // 512
    nc = bass.Bass("TRN2", target_bir_lowering=False)

    def dram(name, shape, dtype, kind):
        return nc.dram_tensor(name, shape, dtype, kind=kind).ap()

    xw = dram("xw", [L + 256, D], F32, "ExternalInput")
    xh = dram("xh", [HL + 256, D], F32, "ExternalInput")
    xo = dram("xo", [HL, D], F32, "ExternalInput")
    cT_d = dram("cT", [128, 16], F32, "ExternalInput")
    flags_d = dram("flags", [128, 8], F32, "ExternalInput")
    pvec_d = dram("pvec", [128, 48], F32, "ExternalInput")
    bada_d = dram("bada", [128, 3 * D], F32, "ExternalInput")
    wada_d = dram("w_ada", [D, 3 * D], F32, "ExternalInput")
    win_d = dram("w_in", [D, NIN], F32, "ExternalInput")
    wa_d = dram("w_a_out", [D, D], F32, "ExternalInput")
    wb_d = dram("w_b_out", [D, D], F32, "ExternalInput")
    wo_d = dram("w_out", [D, D], F32, "ExternalInput")
    wg_d = dram("wg", [128, 3 * 512], F32, "ExternalInput")
    wlr_d = dram("wlr", [D, 128], F32, "ExternalInput")
    fg_d = dram("fg", [128, D], F32, "ExternalInput")
    yw = dram("yw", [L, D], F32, "ExternalOutput")
    yh = dram("yh", [HL, D], F32, "ExternalOutput")
    wbf_in = dram("wbf_in", [D, NIN], BF16, "Internal")
    wbf_a = dram("wbf_a", [D, D], BF16, "Internal")
    wbf_b = dram("wbf_b", [D, D], BF16, "Internal")
    wbf_o = dram("wbf_o", [D, D], BF16, "Internal")
    wbf_ada = dram("wbf_ada", [D, 3 * D], BF16, "Internal")
    sbw_d = dram("sbw", [L // 128, 128, D], BF16, "Internal")
    sbh_d = dram("sbh", [HL // 128, 128, D], BF16, "Internal")
    ksw_d = dram("ksw", [NTW, 128, 4 * 512], BF16, "Internal")
    vsw_d = dram("vsw", [NTW, 128, 4 * D], BF16, "Internal")
    ksh_d = dram("ksh", [NTH, 128, 4 * 512], BF16, "Internal")
    vsh_d = dram("vsh", [NTH, 128, 4 * D], BF16, "Internal")

    with ExitStack() as st:
        P = Prog(nc, st)

        def sb(name, shape, dt):
            return st.enter_context(nc.sbuf_tensor(name, shape, dt))

        def ps(name, shape, dt):
            return st.enter_context(nc.psum_tensor(name, shape, dt))

        NWB = 4
        wbuf = [sb("wbuf%d" % i, [128, 8, 512], BF16) for i in range(NWB)]
        xp = [sb("xp%d" % i, [128, D], F32) for i in range(2)]
        xr = [sb("xr%d" % i, [128, D], F32) for i in range(2)]
        hT = [sb("hT%d" % i, [128, 8, 513], BF16) for i in range(2)]
        xn = [sb("xn%d" % i, [128, D], BF16) for i in range(2)]
        junk = sb("junk", [128, D], BF16)
        stat = sb("stat", [128, 64], F32)
        pbuf = sb("pbuf", [128, 8, 514], BF16)
        B = [sb("B%d" % i, [128, 8, 512], BF16) for i in range(4)]
        ct = [sb("ct%d" % i, [128, 512], F32) for i in range(3)]
        cu = [sb("cu%d" % i, [128, 512], F32) for i in range(2)]
        qT = sb("qT", [128, 4, 512], BF16)
        kT = sb("kT", [128, 4, 512], BF16)
        vsb = sb("vsb", [128, 4, D], BF16)
        lrT = sb("lrT", [128, 512], BF16)
        res = [sb("res%d" % i, [128, D], F32) for i in range(2)]
        E4 = [sb("E%d" % i, [128, 4, 128], F32) for i in range(4)]
        qk4 = [sb("qk%d" % i, [128, 4, 128], BF16) for i in range(4)]
        tmpA = sb("tmpA", [128, 4, 128], F32)
        Abf = sb("Abf", [128, 4, 128], BF16)
        kTT = sb("kTT", [128, 4, 128], BF16)
        onb = sb("onb", [128, D], BF16)
        tmpS = sb("tmpS", [128, D], F32)
        Sf = sb("Sf", [128, D], F32)
        Sb = sb("Sb", [128, D], F32)
        Sfbf = sb("Sfbf", [128, D], BF16)
        Sbbf = [sb("Sbbf%d" % i, [128, D], BF16) for i in range(2)]
        ident = sb("ident", [128, 128], BF16)
        identf = sb("identf", [128, 128], F32)
        triF = sb("triF", [128, 128], F32)
        triB = sb("triB", [128, 128], F32)
        maskF = sb("maskF", [128, 4, 128], BF16)
        maskB = sb("maskB", [128, 4, 128], BF16)
        wg = sb("wgs", [128, 3, 512], BF16)
        wlr = sb("wlrs", [128, 8, 128], BF16)
        gate_bc = sb("gate_bc", [128, D], F32)
        fg_bc = sb("fg_bc", [128, D], F32)
        pvec = sb("pvec_s", [128, 48], F32)
        flags = sb("flags_s", [128, 8], F32)
        cTs = sb("cTs", [128, 16], F32)
        e0row = sb("e0row", [128, 128], BF16)
        modv = sb("modv", [128, 2, 16], F32)
        ones = sb("ones", [128, 128], F32)

        tmpb = cu[0]
        modbc = cu[1]
        csb = onb[:].rearrange("p (a b) -> p a b", a=8)
        badas = Abf[:].rearrange("p a b -> p (a b)")
        pj = [ps("pj%d" % i, [128, 512], F32) for i in range(3)]
        pT = ps("pT", [128, 8, 128], BF16)
        pa = [ps("pa%d" % i, [128, 4, 128], F32) for i in range(2)]
        po = ps("po", [128, D], F32)

        op = P.op
        rot = {}

        def nxt(key, n):
            v = rot.get(key, 0)
            rot[key] = v + 1
            return v % n

        def mm(out, lhsT, rhs, start, stop, reads, writes):
            op('pe', lambda e: e.matmul(out, lhsT=lhsT, rhs=rhs, start=start, stop=stop), reads=reads, writes=writes)

        def tr(out, in_, reads, writes, idn=None):
            idn = ident if idn is None else idn
            op('pe', lambda e: e.transpose(out=out, in_=in_, identity=idn[:]), reads=list(reads) + ['const'],
               writes=writes)

        def act(out, in_, func, reads, writes, scale=1.0, bias=0.0, accum=None):
            if accum is None:
                op('act', lambda e: e.activation(out=out, in_=in_, func=func, scale=scale, bias=bias),
                   reads=reads, writes=writes)
            else:
                op('act', lambda e: e.activation(out=out, in_=in_, func=func, scale=scale, bias=bias,
                                                 accum_out=accum), reads=reads, writes=writes)

        def tt(eng, out, in0, in1, alu, reads, writes):
            op(eng, lambda e: e.tensor_tensor(out=out, in0=in0, in1=in1, op=alu), reads=reads, writes=writes)

        def ts(eng, out, in0, s1, s2, op0, op1, reads, writes):
            if s2 is None:
                op(eng, lambda e: e.tensor_scalar(out=out, in0=in0, scalar1=s1, scalar2=None, op0=op0),
                   reads=reads, writes=writes)
            else:
                op(eng, lambda e: e.tensor_scalar(out=out, in0=in0, scalar1=s1, scalar2=s2, op0=op0, op1=op1),
                   reads=reads, writes=writes)

        def stt(eng, out, in0, scalar, in1, op0, op1, reads, writes):
            op(eng, lambda e: e.scalar_tensor_tensor(out=out, in0=in0, scalar=scalar, in1=in1, op0=op0, op1=op1),
               reads=reads, writes=writes)

        def cp(eng, out, in_, reads, writes):
            if eng == 'act':
                act(out, in_, AF.Copy, reads, writes)
            else:
                op(eng, lambda e: e.tensor_copy(out=out, in_=in_), reads=reads, writes=writes)

        def dma(eng, out, in_, sem, reads, writes):
            op(eng, lambda e: e.dma_start(out=out, in_=in_), reads=reads, writes=writes, dma=sem)

        def rstd_from_ssq(dst, src, nelem, reads, writes):
            act(dst, src, AF.Ln, reads, list(writes), bias=nelem * EPS)
            act(dst, dst, AF.Exp, list(writes), writes, scale=-0.5)

        def affsel(t, cmp, fill, cm, pattern):
            op('pool', lambda e: e.affine_select(out=t, in_=t, compare_op=cmp, fill=fill, base=0,
                                                 pattern=pattern, channel_multiplier=cm),
               reads=['const'], writes=['const'])

        def mset(t, v):
            op('pool', lambda e: e.memset(t, v), writes=['const'])

        mset(ident[:], 0.0)
        affsel(ident[:], ALU.not_equal, 1.0, 1, [[-1, 128]])
        mset(identf[:], 0.0)
        affsel(identf[:], ALU.not_equal, 1.0, 1, [[-1, 128]])
        mset(triF[:], -1.0 / 16)
        affsel(triF[:], ALU.is_ge, 0.0, -1, [[1, 128]])
        mset(triB[:], -1.0 / 16)
        affsel(triB[:], ALU.is_ge, 0.0, 1, [[-1, 128]])
        mset(maskF[:], 1.0)
        affsel(maskF[:], ALU.is_ge, 0.0, -1, [[0, 4], [1, 128]])
        mset(maskB[:], 1.0)
        affsel(maskB[:], ALU.is_gt, 0.0, 1, [[0, 4], [-1, 128]])
        mset(ones[:], 1.0)
        mset(e0row[:], 1.0)
        affsel(e0row[:], ALU.is_ge, 0.0, -1, [[0, 128]])
        mset(lrT[:], 0.0)
        mset(lrT[32:33, :], 1.0)
        mset(pbuf[:], 0.0)
        for b_ in range(2):
            mset(hT[b_][:], 0.0)
        mset(stat[:], 0.0)

        dma('sp', pvec[:], pvec_d, 'c0', [], ['pvec'])
        ts('dve', pvec[:, 40:48], pvec[:, 40:48], 16.0, None, ALU.mult, None, ['pvec'], ['pvec'])
        dma('sp', flags[:], flags_d, 'c1', [], ['flags'])
        dma('sp', cTs[:], cT_d, 'c2', [], ['cts0', 'cts1'])
        dma('sp', fg_bc[:], fg_d, 'c3', [], ['fg'])
        dma('pool', wg[:], wg_d.rearrange("p (a n) -> p a n", a=3), 'c4', [], ['wg'])
        dma('pool', wlr[:], wlr_d.rearrange("(kc p) n -> p kc n", p=128), 'c5', [], ['wlr'])

        ncast = [0]

        def cast(dst, src, ncols, resname):
            c0 = 0
            while c0 < ncols:
                w_ = min(1024, ncols - c0)
                dma('pool', dst[:, c0:c0 + w_], src[:, c0:c0 + w_], 'wc%d' % (ncast[0] % 4), [], [resname])
                ncast[0] += 1
                c0 += w_

        cast(wbf_ada, wada_d, 3 * D, 'wbf')
        cast(wbf_in, win_d, NIN, 'wbf')
        cast(wbf_a, wa_d, D, 'wbf')
        cast(wbf_b, wb_d, D, 'wbf')
        cast(wbf_o, wo_d, D, 'wbf')
        P.wait_all_dma('pool', ['wc0', 'wc1', 'wc2', 'wc3'])
        op('pool', lambda e: e.memset(stat[:, 63:64], 0.0), reads=[], writes=['wbf_ready'])

        wstate = dict(n=0)

        def wload(src, c0, ncols=512):
            i = wstate['n'] % NWB
            wstate['n'] += 1
            r = 'wbuf%d' % i
            dma('sp', wbuf[i][:, :, 0:ncols], src[:, c0:c0 + ncols].rearrange("(kc p) n -> p kc n", p=128),
                'wl%d' % i, ['wbf_ready'], [r])
            return wbuf[i], r

        class WStream:
            def __init__(self, units):
                self.units = units
                self.loaded = []
                self.pos = 0
                for _ in range(min(2, len(units))):
                    self._issue()

            def _issue(self):
                k = len(self.loaded)
                if k < len(self.units):
                    self.loaded.append(wload(*self.units[k]))

            def get(self):
                r = self.loaded[self.pos]
                self.pos += 1
                self._issue()
                return r

        def adaln(s, ws):
            act(cTs[:, s * 8:(s + 1) * 8], cTs[:, s * 8:(s + 1) * 8], AF.Silu, ['cts%d' % s], ['cts%d' % s])
            for kc in range(8):
                act(csb[:, kc, :], ones[:], AF.Copy, ['const', 'cts%d' % s], ['onb'],
                    scale=cTs[:, s * 8 + kc:s * 8 + kc + 1])
            for ch in range(6):
                wt, wr = ws.get()
                dma('pool', badas, bada_d[:, ch * 512:(ch + 1) * 512], 'bd', [], ['Abf'])
                bank = nxt('pj', 3)
                for kc in range(8):
                    mm(pj[bank][:], csb[:, kc, :], wt[:, kc, :], kc == 0, False, ['onb', wr], ['pj%d' % bank])
                mm(pj[bank][:], e0row[:], badas, False, True, ['const', 'Abf'], ['pj%d' % bank])
                if ch >= 4:
                    cp('dve', gate_bc[:, (ch - 4) * 512:(ch - 3) * 512], pj[bank][:], ['pj%d' % bank], ['gate_bc'])
                else:
                    cp('dve', modbc[:], pj[bank][:], ['pj%d' % bank], ['cu1'])
                    for q_ in range(4):
                        kc = (ch % 2) * 4 + q_
                        op('pe', lambda e, q_=q_: e.transpose(out=pa[1][:, q_, :], in_=modbc[:, q_ * 128:(q_ + 1) * 128],
                                                              identity=identf[:]),
                           reads=['cu1', 'const'], writes=['pa1'])
                        if ch < 2:
                            cp('dve', modv[:, s, 8 + kc:9 + kc], pa[1][:, q_, 0:1], ['pa1'], ['modv%d' % s])
                        else:
                            cp('dve', modv[:, s, kc:kc + 1], pa[1][:, q_, 0:1], ['pa1'], ['modv%d' % s])
            ts('dve', modv[:, s, 0:8], modv[:, s, 0:8], 1.0, 32.0, ALU.add, ALU.mult, ['modv%d' % s], ['modv%d' % s])
            tt('dve', modv[:, s, 0:8], modv[:, s, 0:8], pvec[:, 0:8], ALU.mult, ['modv%d' % s, 'pvec'], ['modv%d' % s])

        def prepA(src_rows):
            i = nxt('xp', 2)
            xb = xp[i]
            xr_ = 'xp%d' % i
            dma('sp', xb[:], src_rows, 'xl%d' % i, [], [xr_])
            k = nxt('st', 16)
            ssq = stat[:, k:k + 1]
            rs = stat[:, 16 + k:17 + k]
            act(junk[:], xb[:], AF.Square, [xr_], ['junk', 'ssq%d' % k], accum=ssq)
            rstd_from_ssq(rs, ssq, D, ['ssq%d' % k], ['rs%d' % k])
            n = nxt('xn', 2)
            ts('dve', xn[n][:], xb[:], rs, None, ALU.mult, None, [xr_, 'rs%d' % k], ['xn%d' % n])
            return n

        def prepB(n, hb, j, s):
            for kc in range(8):
                tr(pT[:, kc, :], xn[n][:, kc * 128:(kc + 1) * 128], ['xn%d' % n], ['pT'])
            hr = 'hT%d_%d' % (hb, j)
            for kc in range(8):
                o_ = hT[hb][:, kc, j * 128:(j + 1) * 128]
                if j % 2 == 0:
                    act(o_, pT[:, kc, :], AF.Identity, ['pT', 'modv%d' % s], [hr],
                        scale=modv[:, s, kc:kc + 1], bias=modv[:, s, 8 + kc:9 + kc])
                else:
                    ts('dve', o_, pT[:, kc, :], modv[:, s, kc:kc + 1], modv[:, s, 8 + kc:9 + kc], ALU.mult, ALU.add,
                       ['pT', 'modv%d' % s], [hr])

        def prep_sub(src_rows, hb, j, s):
            prepB(prepA(src_rows), hb, j, s)

        def prep_stages(items, s):
            ctx = {}
            n_ = len(items)
            out = []

            def mkA(i):
                def f():
                    ctx[i] = prepA(items[i][0])
                return f

            def mkB(i):
                def f():
                    prepB(ctx[i], items[i][1], items[i][2], s)
                return f

            def seq(fs):
                def f():
                    for g_ in fs:
                        g_()
                return f
            out.append(seq([mkA(i) for i in range(min(2, n_))]))
            for i in range(n_):
                fs = [mkB(i)]
                if i + 2 < n_:
                    fs.append(mkA(i + 2))
                out.append(seq(fs))
            return out

        def hreads(hb):
            return ['hT%d_%d' % (hb, j) for j in range(4)]

        def proj_fm(ws, hb, off, evac, nchunks=4, hook=None):
            wt, wr = ws.get()
            for c in range(nchunks):
                bank = nxt('pj', 3)
                rd = hreads(hb) + [wr] + (['hT%d_L' % hb] if off else [])
                for kc in range(8):
                    mm(pj[bank][:], wt[:, kc, c * 128:(c + 1) * 128], hT[hb][:, kc, off:off + 512], kc == 0, kc == 7,
                       rd, ['pj%d' % bank])
                evac(c, pj[bank], 'pj%d' % bank)
                if hook is not None:
                    hook()

        def proj_v(ws, hb, vdst=None, vres=None, hook=None):
            vdst = vsb if vdst is None else vdst
            for half in range(2):
                wt, wr = ws.get()
                for j in range(4):
                    bank = nxt('pj', 3)
                    for kc in range(8):
                        mm(pj[bank][:], hT[hb][:, kc, j * 128:(j + 1) * 128], wt[:, kc, :], kc == 0, kc == 7,
                           ['hT%d_%d' % (hb, j), wr], ['pj%d' % bank])
                    cp('act' if j % 2 == 0 else 'dve', vdst[:, j, half * 512:(half + 1) * 512], pj[bank][:],
                       ['pj%d' % bank], ['v%d' % j if vres is None else vres])
                    if hook is not None:
                        hook()

        def proj_lr(hb, ldst=None, lres='lrT'):
            ldst = lrT if ldst is None else ldst
            bank = nxt('pj', 3)
            for kc in range(8):
                mm(pj[bank][:], wlr[:, kc, :], hT[hb][:, kc, 0:512], kc == 0, kc == 7, hreads(hb) + ['wlr'],
                   ['pj%d' % bank])
            cp('dve', ldst[0:32, :], pj[bank][0:32, :], ['pj%d' % bank], [lres])

        kT2 = B[0][:, 0:4, :]
        vsb2 = B[1][:].rearrange("p (a b) c -> p a (b c)", a=4)
        lrT2 = B[2][:, 0, :]

        def state_front(ws, src, row0, s, bs, kv=None):
            hb = nxt('hb', 2)
            kd, kr = (kT, 'kT') if bs == 0 else (kT2, 'B0')
            vd, vr = (vsb, None) if bs == 0 else (vsb2, 'B1')
            ld, lr_ = (lrT, 'lrT') if bs == 0 else (lrT2, 'B2')
            fs = prep_stages([(src[row0 + j * 128:row0 + (j + 1) * 128, :], hb, j) for j in range(4)], s)

            def ev_k(c, bank, br):
                cp('act' if c % 2 == 0 else 'dve', kd[:, c, :], bank[:], [br], [kr])
            def f_k():
                proj_fm(ws, hb, 0, ev_k)
                if kv is not None:
                    ks_d, vs_d, kvname, ti_ = kv
                    dma('pool', ks_d[ti_].rearrange("p (a b) -> p a b", a=4), kd[:, :, :], 'kso', [kr],
                        ['ksd_%s_%d' % (kvname, ti_)])

            def f_v():
                proj_v(ws, hb, vd, vr)
                if kv is not None:
                    ks_d, vs_d, kvname, ti_ = kv
                    dma('pool', vs_d[ti_].rearrange("p (a b) -> p a b", a=4), vd[:, :, :], 'vso',
                        ['v0', 'v1', 'v2', 'v3'] if bs == 0 else ['B1'], ['vsd_%s_%d' % (kvname, ti_)])
            fs.append(f_k)
            fs.append(f_v)
            fs.append(lambda: proj_lr(hb, ld, lr_))
            return fs

        e0t = sb("e0t", [128, 16], F32)
        spx = [res[0][:, 0:512], res[0][:, 512:1024], res[1][:, 0:512], res[1][:, 512:1024]]
        spr = ['res0', 'res0', 'res1', 'res1']
        Eix = [xr[0][:, 0:512], xr[0][:, 512:1024], xr[1][:, 0:512], xr[1][:, 512:1024]]
        Eir = ['xr0', 'xr0', 'xr1', 'xr1']
        kTTs = [B[3][:, cc_, :].rearrange("p (a b) -> p a b", a=4) for cc_ in range(4)]
        cumb = [(pa[0][:], 'pa0'), (pa[1][:], 'pa1')]

        def state_back(bs, widx, store_d, sbname, chunk0, pre_hook=None):
            kd, kr = (kT, 'kT') if bs == 0 else (kT2, 'B0')
            vd = vsb if bs == 0 else vsb2
            ld, lr_ = (lrT, 'lrT') if bs == 0 else (lrT2, 'B2')
            if pre_hook is not None:
                pre_hook()
            order = (3, 2, 1, 0)
            for i, cc in enumerate(order):
                tk = slice(cc * 128, (cc + 1) * 128)
                pb = pa[i % 2][:].rearrange("p a b -> p (a b)")
                pr = 'pa%d' % (i % 2)
                mm(pb, ld[:, tk], wg[:, widx, :], True, True, [lr_, 'wg', 'const'], [pr])
                act(spx[cc], pb, AF.Exp, [pr], [spr[cc]], scale=-1.0)
                act(spx[cc], spx[cc], AF.Ln, [spr[cc]], [spr[cc]], bias=1.0)
                if i % 2 == 1:
                    yield
            for i, cc in enumerate(order):
                cb, cr = cumb[i % 2]
                for h in range(4):
                    mm(cb[:, h, :], spx[cc][:, h * 128:(h + 1) * 128], triB[:], True, True,
                       [spr[cc], 'const'], [cr])
                act(Eix[cc].rearrange("p (a b) -> p a b", a=4), cb, AF.Exp, [cr], [Eir[cc]],
                    scale=-1.0)
                act(e0t[:, cc * 4:(cc + 1) * 4], cb[:, :, 0], AF.Exp, [cr], ['e0t%d' % cc])
                tt('dve', qk4[cc][:], kd[:, :, cc * 128:(cc + 1) * 128], Eix[cc].rearrange("p (a b) -> p a b", a=4),
                   ALU.mult, [kr, Eir[cc]], ['qk%d' % cc])
                yield
            for pair in ((3, 2), (1, 0)):
                for q_, cc in enumerate(pair):
                    for h in range(4):
                        tr(pT[:, q_ * 4 + h, :], qk4[cc][:, h, :], ['qk%d' % cc], ['pT'])
                for q_, cc in enumerate(pair):
                    cp('act' if q_ == 0 else 'dve', kTTs[cc], pT[:, q_ * 4:(q_ + 1) * 4, :], ['pT'], ['B3'])
                yield
            cur, cur_r, nx_, nx_r = Sb, 'Sb', tmpS, 'tmpS'
            for cc in order:
                vr = ('v%d' % cc) if bs == 0 else 'B1'
                if store_d is not None:
                    sl = nxt('sbs', 2)
                    cp('act', Sbbf[sl][:], cur[:], [cur_r], ['Sbbf%d' % sl])
                    dma('pool', store_d[chunk0 + cc], Sbbf[sl][:], 'sbo%d' % sl, ['Sbbf%d' % sl],
                        ['sbd_%s_%d' % (sbname, chunk0 + cc)])
                for h in range(4):
                    mm(po[:, h * 256:(h + 1) * 256], kTTs[cc][:, h, :], vd[:, cc, h * 256:(h + 1) * 256], True, True,
                       ['B3', vr], ['po'])
                for h in range(4):
                    ts('dve', nx_[:, h * 256:(h + 1) * 256], cur[:, h * 256:(h + 1) * 256],
                       e0t[:, cc * 4 + h:cc * 4 + h + 1], None, ALU.mult, None, [cur_r, 'e0t%d' % cc], [nx_r])
                for h in range(4):
                    stt('dve', nx_[:, h * 256:(h + 1) * 256], po[:, h * 256:(h + 1) * 256],
                        e0t[:, cc * 4 + h:cc * 4 + h + 1], nx_[:, h * 256:(h + 1) * 256], ALU.mult, ALU.add,
                        ['po', 'e0t%d' % cc, nx_r], [nx_r])
                cur, cur_r, nx_, nx_r = nx_, nx_r, cur, cur_r
                yield

        def zip_run(fronts, back, nback):
            nf = len(fronts)
            done = 0
            alive = back is not None
            for i, f in enumerate(fronts):
                f()
                want = ((i + 1) * nback + nf - 1) // nf if nf else nback
                while alive and done < want:
                    try:
                        next(back)
                        done += 1
                    except StopIteration:
                        alive = False
            if alive:
                for _ in back:
                    pass

        def state_sweep(ws, s, tiles):
            op('pool', lambda e: e.memset(lrT2, 0.0), writes=['B2'])
            op('pool', lambda e: e.memset(B[2][32:33, 0, :], 1.0), writes=['B2'])
            n = len(tiles)
            if n == 0:
                return
            hbs = []

            def mk_prep(i):
                t = tiles[i]
                hb = nxt('hb', 2)
                hbs.append(hb)
                return prep_stages([(t['src'][t['row0'] + j * 128:t['row0'] + (j + 1) * 128, :], hb, j)
                                    for j in range(4)], s)
            for f in mk_prep(0):
                f()
            prev = None
            for i, t in enumerate(tiles):
                bs = i % 2
                hb = hbs[i]
                kd, kr = (kT, 'kT') if bs == 0 else (kT2, 'B0')
                vd, vr = (vsb, None) if bs == 0 else (vsb2, 'B1')
                ld, lr_ = (lrT, 'lrT') if bs == 0 else (lrT2, 'B2')
                stages = []
                pst = mk_prep(i + 1) if i + 1 < n else []
                bst = []
                if prev is not None:
                    g_ = prev
                    bst = [(lambda g_=g_: next(g_, None)) for _ in range(12)]
                nb, npst = len(bst), len(pst)
                tot = nb + npst
                ib = ip = 0
                for k_ in range(tot):
                    if ip < npst and (ib >= nb or ip * tot <= k_ * npst):
                        stages.append(pst[ip])
                        ip += 1
                    else:
                        stages.append(bst[ib])
                        ib += 1
                st = dict(slot=0, done=0)
                NSL = 12

                def tick(st=st, stages=stages):
                    st['slot'] += 1
                    want = min(len(stages), (st['slot'] * len(stages) + NSL - 1) // NSL)
                    while st['done'] < want:
                        stages[st['done']]()
                        st['done'] += 1

                def ev_k(c, bank, br, kd=kd, kr=kr):
                    cp('act' if c % 2 == 0 else 'dve', kd[:, c, :], bank[:], [br], [kr])
                proj_fm(ws, hb, 0, ev_k, hook=tick)
                kv = t.get('kv')
                if kv is not None:
                    ks_d, vs_d, kvname, ti_ = kv
                    dma('pool', ks_d[ti_].rearrange("p (a b) -> p a b", a=4), kd[:, :, :], 'kso', [kr],
                        ['ksd_%s_%d' % (kvname, ti_)])
                proj_v(ws, hb, vd, vr, hook=tick)
                if kv is not None:
                    dma('pool', vs_d[ti_].rearrange("p (a b) -> p a b", a=4), vd[:, :, :], 'vso',
                        ['v0', 'v1', 'v2', 'v3'] if bs == 0 else ['B1'], ['vsd_%s_%d' % (kvname, ti_)])
                proj_lr(hb, ld, lr_)
                while st['done'] < len(stages):
                    stages[st['done']]()
                    st['done'] += 1
                prev = state_back(bs, t['widx'], t['store_d'], t['sbname'], t['chunk0'], t.get('pre_hook'))
            if prev is not None:
                for _ in prev:
                    pass

        def main_tile(ws, src, row0, s, ti, ntiles, hb, sb_d, sbname, chunk0, ydst, yrow0, flagR_col, side=(), kv=None):
            def fm_unit(evac, off=0):
                return lambda: proj_fm(ws, hb, off, evac)

            def fmB_unit(Bt, Br, evac):
                return lambda: proj_fm_B(ws, Bt, Br, evac)

            def gla_stages(cc):
                tk = slice(cc * 128, (cc + 1) * 128)
                ctx = {}
                g1, g2 = 0, 1

                def y1():
                    pf = pa[0][:].rearrange("p a b -> p (a b)")
                    pb_ = pa[1][:].rearrange("p a b -> p (a b)")
                    mm(pf, lrT[:, tk], wg[:, 0, :], True, True, ['lrT', 'wg', 'const'], ['pa0'])
                    mm(pb_, lrT[:, tk], wg[:, 1, :], True, True, ['lrT', 'wg', 'const'], ['pa1'])
                    act(cu[g1][:], pf, AF.Exp, ['pa0'], ['cu%d' % g1], scale=-1.0)
                    act(cu[g1][:], cu[g1][:], AF.Ln, ['cu%d' % g1], ['cu%d' % g1], bias=1.0)
                    act(cu[g2][:], pb_, AF.Exp, ['pa1'], ['cu%d' % g2], scale=-1.0)
                    act(cu[g2][:], cu[g2][:], AF.Ln, ['cu%d' % g2], ['cu%d' % g2], bias=1.0)

                def y2():
                    for h in range(4):
                        mm(pa[0][:, h, :], cu[g1][:, h * 128:(h + 1) * 128], triF[:], True, True,
                           ['cu%d' % g1, 'const'], ['pa0'])
                    act(E4[0][:], pa[0][:], AF.Exp, ['pa0'], ['E0'])
                    act(E4[1][:], pa[0][:], AF.Exp, ['pa0'], ['E1'], scale=-1.0)
                    act(e0t[:, cc * 4:(cc + 1) * 4], pa[0][:, :, 127], AF.Exp, ['pa0'], ['e0t%d' % cc])
                    tt('dve', qk4[0][:], qT[:, :, tk], E4[0][:], ALU.mult, ['qT', 'E0'], ['qk0'])
                    tt('dve', qk4[1][:], kT[:, :, tk], E4[1][:], ALU.mult, ['kT', 'E1'], ['qk1'])

                def y3():
                    for h in range(4):
                        mm(pa[1][:, h, :], cu[g2][:, h * 128:(h + 1) * 128], triB[:], True, True,
                           ['cu%d' % g2, 'const'], ['pa1'])
                    act(E4[2][:], pa[1][:], AF.Exp, ['pa1'], ['E2'])
                    act(E4[3][:], pa[1][:], AF.Exp, ['pa1'], ['E3'], scale=-1.0)
                    tt('dve', qk4[2][:], qT[:, :, tk], E4[2][:], ALU.mult, ['qT', 'E2'], ['qk2'])
                    tt('dve', qk4[3][:], kT[:, :, tk], E4[3][:], ALU.mult, ['kT', 'E3'], ['qk3'])
                    sl = nxt('sbs', 2)
                    ctx['sl'] = sl
                    dma('sp', Sbbf[sl][:], sb_d[chunk0 + cc], 'sbi%d' % sl,
                        ['sbd_%s_%d' % (sbname, chunk0 + cc)], ['Sbbf%d' % sl])

                def y4a():
                    for h in range(4):
                        mm(pa[0][:, h, :], qk4[1][:, h, :], qk4[0][:, h, :], True, True, ['qk0', 'qk1'], ['pa0'])
                    for h in range(4):
                        mm(pa[1][:, h, :], qk4[3][:, h, :], qk4[2][:, h, :], True, True, ['qk2', 'qk3'], ['pa1'])
                    tt('dve', tmpA[:], pa[0][:], maskF[:], ALU.mult, ['pa0', 'const'], ['tmpA'])
                    tt('dve', Abf[:], pa[1][:], maskB[:], ALU.mult, ['pa1', 'const'], ['Abf'])
                    tt('dve', Abf[:], Abf[:], tmpA[:], ALU.add, ['Abf', 'tmpA'], ['Abf'])

                def y4b():
                    for h in range(4):
                        tr(pT[:, h, :], qk4[1][:, h, :], ['qk1'], ['pT'])
                    cp('act', kTT[:], pT[:, 0:4, :], ['pT'], ['kTT'])

                def y5():
                    sl = ctx['sl']
                    for h in range(4):
                        o_ = po[:, h * 256:(h + 1) * 256]
                        mm(o_, Abf[:, h, :], vsb[:, cc, h * 256:(h + 1) * 256], True, False, ['Abf', 'v%d' % cc], ['po'])
                        mm(o_, qk4[0][:, h, :], Sfbf[:, h * 256:(h + 1) * 256], False, False, ['qk0', 'Sfbf'], ['po'])
                        mm(o_, qk4[2][:, h, :], Sbbf[sl][:, h * 256:(h + 1) * 256], False, True,
                           ['qk2', 'Sbbf%d' % sl], ['po'])
                    k = nxt('st4', 4)
                    ssq4 = stat[:, 32 + 4 * k:36 + 4 * k]
                    rs4 = stat[:, 48 + 4 * k:52 + 4 * k]
                    for h in range(4):
                        act(junk[:, h * 256:(h + 1) * 256], po[:, h * 256:(h + 1) * 256], AF.Square, ['po'],
                            ['junk', 'ssq4_%d' % k], accum=ssq4[:, h:h + 1])
                    rstd_from_ssq(rs4, ssq4, 256, ['ssq4_%d' % k], ['rs4_%d' % k])
                    for h in range(4):
                        ts('dve', onb[:, h * 256:(h + 1) * 256], po[:, h * 256:(h + 1) * 256], rs4[:, h:h + 1], None,
                           ALU.mult, None, ['po', 'rs4_%d' % k], ['onb'])

                def y6():
                    for kc in range(8):
                        tr(pT[:, kc, :], onb[:, kc * 128:(kc + 1) * 128], ['onb'], ['pT'])
                    for h in range(4):
                        mm(po[:, h * 256:(h + 1) * 256], kTT[:, h, :], vsb[:, cc, h * 256:(h + 1) * 256], True, True,
                           ['kTT', 'v%d' % cc], ['po'])
                    for kc in range(8):
                        stt('dve', B[2][:, kc, tk], pT[:, kc, :], pvec[:, 40 + kc:41 + kc], B[0][:, kc, tk], ALU.mult,
                            ALU.mult, ['pT', 'pvec', 'B0'], ['B2'])
                    tt('dve', tmpS[:], po[:], Sf[:], ALU.add, ['po', 'Sf'], ['tmpS'])
                    for h in range(4):
                        ts('dve', Sfbf[:, h * 256:(h + 1) * 256], tmpS[:, h * 256:(h + 1) * 256],
                           e0t[:, cc * 4 + h:cc * 4 + h + 1], None, ALU.mult, None, ['tmpS', 'e0t%d' % cc], ['Sfbf'])
                    for h in range(4):
                        act(Sf[:, h * 256:(h + 1) * 256], tmpS[:, h * 256:(h + 1) * 256], AF.Copy,
                            ['tmpS', 'e0t%d' % cc], ['Sf'], scale=e0t[:, cc * 4 + h:cc * 4 + h + 1])
                return [y1, y2, y3, y4a, y4b, y5, y6]

            def gla_gen():
                st = [gla_stages(cc) for cc in range(4)]
                seq = st[0][0:6]
                for c in range(4):
                    if c + 1 < 4:
                        seq.append(st[c + 1][0])
                    seq.append(st[c][6])
                    if c + 1 < 4:
                        seq += st[c + 1][1:6]
                assert len(seq) == 28
                for f in seq:
                    f()
                    yield

            def ev_q(c, bank, br):
                act(qT[:, c, :], bank[:], AF.Copy, [br], ['qT'], scale=128 ** -0.5)

            def ev_k(c, bank, br):
                cp('dve', kT[:, c, :], bank[:], [br], ['kT'])

            def mk(kind, half):
                def ev(c, bank, br):
                    cc_ = half * 4 + c
                    if kind == 'bz':
                        act(B[0][:, cc_, :], bank[:], AF.Silu, [br], ['B0'])
                    elif kind == 'az':
                        act(B[1][:, cc_, :], bank[:], AF.Silu, [br], ['B1'])
                    elif kind == 'ab':
                        tt('dve', B[1][:, cc_, :], bank[:], B[1][:, cc_, :], ALU.mult, [br, 'B1'], ['B1'])
                    elif kind == 'ac':
                        act(B[3][:, cc_, :], bank[:], AF.Copy, [br], ['B3'])
                    elif kind == 'ax':
                        tt('dve', pbuf[:, cc_, 2:514], bank[:], B[3][:, cc_, :], ALU.mult, [br, 'B3'],
                           ['pbuf'])
                        if CONV_INLINE:
                            if ti == ntiles - 1:
                                ts('dve', pbuf[:, cc_, 513:514], pbuf[:, cc_, 513:514], flags[:, flagR_col:flagR_col + 1],
                                   None, ALU.mult, None, ['pbuf', 'flags'], ['pbuf'])
                            a0 = nxt('ct', 3)
                            act(ct[a0][:], pbuf[:, cc_, 1:513], AF.Identity, ['pbuf', 'pvec'], ['ct%d' % a0],
                                scale=pvec[:, 16 + cc_:17 + cc_], bias=pvec[:, 32 + cc_:33 + cc_])
                            stt('dve', ct[a0][:], pbuf[:, cc_, 0:512], pvec[:, 8 + cc_:9 + cc_], ct[a0][:], ALU.mult, ALU.add,
                                ['pbuf', 'pvec', 'ct%d' % a0], ['ct%d' % a0])
                            stt('dve', ct[a0][:], pbuf[:, cc_, 2:514], pvec[:, 24 + cc_:25 + cc_], ct[a0][:], ALU.mult,
                                ALU.add, ['pbuf', 'pvec', 'ct%d' % a0], ['ct%d' % a0])
                            tt('pool', B[3][:, cc_, :], ct[a0][:], B[1][:, cc_, :], ALU.mult, ['ct%d' % a0, 'B1'],
                               ['B3'])
                            cp('pool', pbuf[:, cc_, 0:2], pbuf[:, cc_, 512:514], ['pbuf'], ['pbuf'])
                    elif kind == 'ma':
                        act(B[1][:, cc_, :], bank[:], AF.Sigmoid, [br], ['B1'])
                    elif kind == 'wa':
                        tt('dve', B[1][:, cc_, :], bank[:], B[1][:, cc_, :], ALU.mult, [br, 'B1'], ['B1'])
                    elif kind == 'mb':
                        act(B[3][:, cc_, :], bank[:], AF.Sigmoid, [br], ['B3'])
                    elif kind == 'wb':
                        tt('dve', tmpb[:], bank[:], B[3][:, cc_, :], ALU.mult, [br, 'B3'], ['cu0'])
                        tt('dve', B[1][:, cc_, :], B[1][:, cc_, :], tmpb[:], ALU.add, ['B1', 'cu0'], ['B1'])
                return ev


            def conv_unit():
                PB = ['pbuf']
                if ti == ntiles - 1:
                    ts('dve', pbuf[:, :, 513:514], pbuf[:, :, 513:514], flags[:, flagR_col:flagR_col + 1], None,
                       ALU.mult, None, PB + ['flags'], PB)
                for c in range(8):
                    a0 = nxt('ct', 3)
                    act(ct[a0][:], pbuf[:, c, 1:513], AF.Identity, PB + ['pvec'], ['ct%d' % a0],
                        scale=pvec[:, 16 + c:17 + c], bias=pvec[:, 32 + c:33 + c])
                    stt('dve', ct[a0][:], pbuf[:, c, 0:512], pvec[:, 8 + c:9 + c], ct[a0][:], ALU.mult, ALU.add,
                        PB + ['pvec', 'ct%d' % a0], ['ct%d' % a0])
                    stt('dve', ct[a0][:], pbuf[:, c, 2:514], pvec[:, 24 + c:25 + c], ct[a0][:], ALU.mult, ALU.add,
                        PB + ['pvec', 'ct%d' % a0], ['ct%d' % a0])
                    tt('dve', B[3][:, c, :], ct[a0][:], B[1][:, c, :], ALU.mult, ['ct%d' % a0, 'B1'], ['B3'])
                    tick()
                cp('act', pbuf[:, :, 0:2], pbuf[:, :, 512:514], PB, PB)

            B3all = ['B3']
            g = gla_gen()
            NST = 28
            state = dict(done=0, slot=0, alive=True)
            NSLOT = 54

            def tick():
                state['slot'] += 1
                want = (state['slot'] * NST + NSLOT - 1) // NSLOT
                while state['alive'] and state['done'] < want:
                    try:
                        next(g)
                        state['done'] += 1
                    except StopIteration:
                        state['alive'] = False

            def fmu(kind, half, off=0):
                return lambda: proj_fm(ws, hb, off, mk(kind, half), hook=tick)

            ks_d, vs_d, kvname = kv
            dma('sp', kT[:], ks_d[ti].rearrange("p (a b) -> p a b", a=4), 'ksi', ['ksd_%s_%d' % (kvname, ti)], ['kT'])
            dma('sp', vsb[:], vs_d[ti].rearrange("p (a b) -> p a b", a=4), 'vsi', ['vsd_%s_%d' % (kvname, ti)],
                ['v0', 'v1', 'v2', 'v3'])
            head = [lambda: proj_fm(ws, hb, 0, ev_q),
                    lambda: proj_lr(hb), lambda: proj_fm(ws, hb, 0, mk('bz', 0)), lambda: proj_fm(ws, hb, 0, mk('bz', 1))]
            mid = [fmu('az', 0), fmu('az', 1), fmu('ab', 0), fmu('ab', 1), fmu('ac', 0, 1), fmu('ac', 1, 1),
                   fmu('ax', 0, 1), fmu('ax', 1, 1)] + ([] if CONV_INLINE else [conv_unit]) + [fmu('ma', 0), fmu('ma', 1),
                   lambda: proj_fm_B(ws, B[3], B3all, mk('wa', 0), hook=tick),
                   lambda: proj_fm_B(ws, B[3], B3all, mk('wa', 1), hook=tick), fmu('mb', 0), fmu('mb', 1)]
            side = list(side)
            for u in head + mid:
                if side:
                    side.pop(0)()
                u()
            while side:
                side.pop(0)()
            for _ in g:
                pass
            proj_fm_B(ws, B[2], 'B2', mk('wb', 0))
            proj_fm_B(ws, B[2], 'B2', mk('wb', 1))
            wts = [ws.get(), ws.get()]
            deferred = None
            for j in range(4):
                ri = nxt('res', 2)
                xi = nxt('xr', 2)
                dma('pool', xr[xi][:], src[row0 + j * 128:row0 + (j + 1) * 128, :], 'xrl%d' % xi, [], ['xr%d' % xi])
                for half in range(2):
                    wt, wr = wts[half]
                    bank = nxt('pj', 3)
                    for kc in range(8):
                        mm(pj[bank][:], B[1][:, kc, j * 128:(j + 1) * 128], wt[:, kc, :], kc == 0, kc == 7, ['B1', wr],
                           ['pj%d' % bank])
                    hs = slice(half * 512, (half + 1) * 512)
                    tt('dve', res[ri][:, hs], pj[bank][:], gate_bc[:, hs], ALU.mult, ['pj%d' % bank, 'gate_bc'],
                       ['res%d' % ri])
                tt('dve', res[ri][:], res[ri][:], xr[xi][:], ALU.add, ['res%d' % ri, 'xr%d' % xi], ['res%d' % ri])
                k = nxt('st', 16)
                ssq = stat[:, k:k + 1]
                rs = stat[:, 16 + k:17 + k]
                act(junk[:], res[ri][:], AF.Square, ['res%d' % ri], ['junk', 'ssq%d' % k], accum=ssq)
                act(rs, ssq, AF.Ln, ['ssq%d' % k], ['rs%d' % k], scale=1.0 / D, bias=EPS)
                act(rs, rs, AF.Exp, ['rs%d' % k], ['rs%d' % k], scale=-0.5)
                if deferred is not None:
                    deferred()

                def fin(ri=ri, k=k, rs=rs, j=j):
                    stt('dve', res[ri][:], res[ri][:], rs, fg_bc[:], ALU.mult, ALU.mult,
                        ['res%d' % ri, 'rs%d' % k, 'fg'], ['res%d' % ri])
                    dma('pool', ydst[yrow0 + j * 128:yrow0 + (j + 1) * 128, :], res[ri][:], 'yo%d' % ri,
                        ['res%d' % ri], [])
                deferred = fin
            return deferred

        def proj_fm_cols(ws, hb, evac):
            proj_fm(ws, hb, 0, evac)

        def proj_fm_B(ws, Bt, Br, evac, hook=None):
            wt, wr = ws.get()
            for c in range(4):
                bank = nxt('pj', 3)
                for kc in range(8):
                    mm(pj[bank][:], wt[:, kc, c * 128:(c + 1) * 128], Bt[:, kc, :], kc == 0, kc == 7,
                       (Br if isinstance(Br, list) else [Br]) + [wr], ['pj%d' % bank])
                evac(c, pj[bank], 'pj%d' % bank)
                if hook is not None:
                    hook()

        def units_state():
            return [(wbf_in, C_K), (wbf_in, C_V), (wbf_in, C_V + 512)]

        def units_pre():
            return [(wbf_in, C_AC), (wbf_in, C_AC + 512), (wbf_in, C_AX), (wbf_in, C_AX + 512)]

        def units_main():
            u = [(wbf_in, C_Q)]
            for base in (C_BZ, C_AZ, C_AB, C_AC, C_AX, C_MA):
                u += [(wbf_in, base), (wbf_in, base + 512)]
            u += [(wbf_a, 0), (wbf_a, 512)]
            u += [(wbf_in, C_MB), (wbf_in, C_MB + 512)]
            u += [(wbf_b, 0), (wbf_b, 512), (wbf_o, 0), (wbf_o, 512)]
            return u

        def units_ada():
            return [(wbf_ada, ch * 512) for ch in range(6)]

        def segment(s, xsrc, n_own_tiles, warm_src, n_warm_tiles, sb_d, sbname, ydst, flagL_col, flagR_col, use_m, kvd):
            units = units_ada()
            units += units_state() * (n_warm_tiles + n_own_tiles)
            units += units_pre()
            units += units_main() * n_own_tiles
            ws = WStream(units)
            adaln(s, ws)
            op('pool', lambda e: e.memset(Sb[:], 0.0), writes=['Sb'])
            op('pool', lambda e: e.memset(Sf[:], 0.0), writes=['Sf'])
            tiles = []
            for t in range(n_warm_tiles - 1, -1, -1):
                tiles.append(dict(src=warm_src, row0=t * 512, widx=2, store_d=None, sbname=None, chunk0=0))

            def m_hook():
                ts('dve', Sf[:], Sb[:], flags[:, 3:4], None, ALU.mult, None, ['Sb', 'flags'], ['Sf'])
                ts('dve', Sb[:], Sb[:], flags[:, 0:1], None, ALU.mult, None, ['Sb', 'flags'], ['Sb'])
            for i_, t in enumerate(range(n_own_tiles - 1, -1, -1)):
                d_ = dict(src=xsrc, row0=128 + t * 512, widx=1, store_d=sb_d, sbname=sbname, chunk0=t * 4,
                          kv=(kvd[0], kvd[1], kvd[2], t))
                if use_m and i_ == 0:
                    d_['pre_hook'] = m_hook
                tiles.append(d_)
            state_sweep(ws, s, tiles)
            cp('pool', Sfbf[:], Sf[:], ['Sf'], ['Sfbf'])
            hb0 = nxt('hb', 2)
            prep_sub(xsrc[0:128, :], hb0, 3, s)
            hb1 = nxt('hb', 2)
            for j in range(4):
                prep_sub(xsrc[128 + j * 128:128 + (j + 1) * 128, :], hb1, j, s)
            cp('pool', hT[hb0][:, :, 512:513], hT[hb1][:, :, 0:1], ['hT%d_0' % hb1], ['hT%d_L' % hb0])
            for half in range(2):
                def ev_ac(c, bank, br, half=half):
                    act(B[3][:, half * 4 + c, :], bank[:], AF.Copy, [br], ['B3'])
                proj_fm(ws, hb0, 1, ev_ac)
            for half in range(2):
                def ev_ax(c, bank, br, half=half):
                    tt('dve', pbuf[:, half * 4 + c, 2:514], bank[:], B[3][:, half * 4 + c, :], ALU.mult, [br, 'B3'],
                       ['pbuf'])
                proj_fm(ws, hb0, 1, ev_ax)
            ts('pool', pbuf[:, :, 512:513], pbuf[:, :, 512:513], flags[:, flagL_col:flagL_col + 1], None, ALU.mult,
               None, ['pbuf'] + ['flags'], ['pbuf'])
            cp('pool', pbuf[:, :, 0:2], pbuf[:, :, 512:514], ['pbuf'], ['pbuf'])
            hb = hb1
            pending = None
            for t in range(n_own_tiles):
                hbn = nxt('hb', 2)
                assert hbn != hb
                if t + 1 < n_own_tiles:
                    items = [(xsrc[128 + (t + 1) * 512 + j * 128:128 + (t + 1) * 512 + (j + 1) * 128, :], hbn, j)
                             for j in range(4)]
                else:
                    r0 = 128 + (t + 1) * 512
                    items = [(xsrc[r0:r0 + 128, :], hbn, 0)]
                side = prep_stages(items, s)

                def la(hb=hb, hbn=hbn):
                    cp('act', hT[hb][:, :, 512:513], hT[hbn][:, :, 0:1], ['hT%d_0' % hbn], ['hT%d_L' % hb])
                side.insert(2, la)
                if pending is not None:
                    side.insert(1, pending)
                pending = main_tile(ws, xsrc, 128 + t * 512, s, t, n_own_tiles, hb, sb_d, sbname, t * 4, ydst, t * 512,
                                    flagR_col, side, kvd)
                hb = hbn
            if pending is not None:
                pending()

        segment(0, xw, NTW, None, 0, sbw_d, 'sbw', yw, 4, 4, False, (ksw_d, vsw_d, 'w'))
        segment(1, xh, NTH, xo, NTH, sbh_d, 'sbh', yh, 1, 2, True, (ksh_d, vsh_d, 'h'))
        P.wait_all_dma('pool', ['yo0', 'yo1'])
        P.emit()
    return nc


_NC_CACHE = {}


def _fm(v):
    return np.ascontiguousarray(np.asarray(v, np.float32).reshape(8, 128).T)


def kernel(**inputs):
    f32 = lambda a: np.ascontiguousarray(np.asarray(a, dtype=np.float32))
    xp_, xs_ = f32(inputs['x_prompt']), f32(inputs['x_sample'])
    cp_, cs_ = f32(inputs['c_prompt']), f32(inputs['c_sample'])
    nP, L, _ = xp_.shape
    nS = xs_.shape[0]
    seqs = [(0, i) for i in range(nP)] + [(1, i) for i in range(nS)]
    nseq = len(seqs)
    assert nseq % 3 == 0
    ncores = 2 * (nseq // 3)
    HL = L // 2

    def X(k):
        g, i = seqs[k]
        return (xp_ if g == 0 else xs_)[i]

    def Cv(k):
        g, i = seqs[k]
        return (cp_ if g == 0 else cs_)[i]

    w_in = f32(inputs['w_in'][0])
    w_ada = f32(inputs['w_ada'][0])
    w_a = f32(inputs['w_a_out'][0])
    w_b = f32(inputs['w_b_out'][0])
    w_o = f32(inputs['w_out'][0])
    conv_w = f32(inputs['conv_w'][0])
    pvec = np.concatenate([_fm(inputs['norm_g'][0]), _fm(conv_w[0]), _fm(conv_w[1]), _fm(conv_w[2]),
                           _fm(inputs['conv_b'][0]), _fm(inputs['gla_norm_g'][0])], axis=1)
    pvec = np.ascontiguousarray(pvec, dtype=np.float32)
    bada = np.zeros((128, 3 * D), np.float32)
    bada[0] = f32(inputs['b_ada'][0])
    wgf = np.zeros((128, 512), np.float32)
    wgf[0:16] = f32(inputs['w_gate_f'][0])
    wgf[32] = f32(inputs['b_gate_f'][0])
    wgb = np.zeros((128, 512), np.float32)
    wgb[16:32] = f32(inputs['w_gate_b'][0])
    wgb[32] = f32(inputs['b_gate_b'][0])
    wlr = np.zeros((D, 128), np.float32)
    wlr[:, 0:32] = w_in[:, C_LR:C_LR + 32]
    fg = np.ascontiguousarray(np.broadcast_to(f32(inputs['final_norm_g'])[None, :], (128, D)))
    zrows = np.zeros((128, D), np.float32)

    in_maps = []
    for core in range(ncores):
        j, odd = core // 2, core % 2
        kw = 3 * j + (2 if odd else 0)
        kb = 3 * j + 1
        xb = X(kb)
        xw_ = np.concatenate([zrows, X(kw), zrows], axis=0)
        if not odd:
            post = zrows.copy()
            post[0] = xb[HL]
            xh_ = np.concatenate([zrows, xb[:HL], post], axis=0)
            xo_ = np.ascontiguousarray(xb[HL:])
            fl = [1.0, 0.0, 1.0, 0.0]
            wgw = wgb
        else:
            pre = zrows.copy()
            pre[127] = xb[HL - 1]
            xh_ = np.concatenate([pre, xb[HL:], zrows], axis=0)
            xo_ = np.ascontiguousarray(xb[:HL][::-1])
            fl = [0.0, 1.0, 0.0, 1.0]
            wgw = wgf
        flags = np.zeros((128, 8), np.float32)
        flags[:, 0:4] = np.asarray(fl, np.float32)[None, :]
        cT = np.concatenate([_fm(Cv(kw)), _fm(Cv(kb))], axis=1)
        wg = np.ascontiguousarray(np.stack([wgf, wgb, wgw], axis=1).reshape(128, 3 * 512))
        in_maps.append(dict(xw=np.ascontiguousarray(xw_), xh=np.ascontiguousarray(xh_), xo=xo_,
                            cT=np.ascontiguousarray(cT, dtype=np.float32), flags=flags, pvec=pvec, bada=bada,
                            w_ada=w_ada, w_in=w_in, w_a_out=w_a, w_b_out=w_b, w_out=w_o, wg=wg, wlr=wlr, fg=fg))
    if L not in _NC_CACHE:
        _NC_CACHE[L] = build(L)
    nc = _NC_CACHE[L]
    res = run_bass_kernel_spmd(nc, in_maps, core_ids=list(range(ncores)))
    outs = [None] * nseq
    for core in range(ncores):
        j, odd = core // 2, core % 2
        r = res.results[core]
        kw = 3 * j + (2 if odd else 0)
        kb = 3 * j + 1
        outs[kw] = np.asarray(r['yw'], np.float32)
        if outs[kb] is None:
            outs[kb] = np.zeros((L, D), np.float32)
        if not odd:
            outs[kb][:HL] = np.asarray(r['yh'], np.float32)
        else:
            outs[kb][HL:] = np.asarray(r['yh'], np.float32)
    y_prompt = np.stack(outs[:nP], axis=0)
    y_sample = np.stack(outs[nP:], axis=0)
    return (y_prompt, y_sample)
```
